# Optimizing a Trainium2 kernel written in Bass

```python
import math
import jax, jax.numpy as jnp
from jax import lax
import numpy as np

D_MODEL = 1024
BATCH = 8
SEQ = 4096
DEPTH = 4

HEAD_DIM = 64
N_FOX_HEADS = 8
N_SB_HEADS = 8
N_MOBA_HEADS = 8
N_MLA_HEADS = 8
Q_BLOCK = 128
MOBA_BLOCK = 256
MOBA_TOPK = 3
MOBA_Q_CHUNK = 32
MLA_Q_LORA = 256
MLA_KV_LORA = 128
MLA_NOPE_DIM = 64
MLA_ROPE_DIM = 32
MLA_V_DIM = 64
MLA_QK_DIM = MLA_NOPE_DIM + MLA_ROPE_DIM
ROPE_BASE = 10000.0
REL_BUCKETS = 32
REL_MAX_EXACT = 16
REL_MAX_DISTANCE = 128
D_FF = ((8 * D_MODEL + 3 * 256 - 1) // (3 * 256)) * 256
RMS_EPS = 1e-6
N_EVEN = (DEPTH + 1) // 2
N_ODD = DEPTH // 2
FOX_W = N_FOX_HEADS * HEAD_DIM
SB_W = N_SB_HEADS * HEAD_DIM
MOBA_W = N_MOBA_HEADS * HEAD_DIM
EVEN_WIDTHS = (FOX_W, FOX_W, FOX_W, N_FOX_HEADS, SB_W, SB_W, SB_W)
ODD_WIDTHS = (MOBA_W, MOBA_W, MOBA_W, MLA_Q_LORA, MLA_KV_LORA, MLA_ROPE_DIM)
EVEN_IN = sum(EVEN_WIDTHS)
ODD_IN = sum(ODD_WIDTHS)
EVEN_MIX = FOX_W + SB_W
ODD_MIX = MOBA_W + N_MLA_HEADS * MLA_V_DIM

kernel_name = "hybrid_fox_stickbreak_moba_mla_trunk"


def _rms_norm(t, gain):
    tf = t.astype(jnp.float32)
    tf = tf * lax.rsqrt(jnp.mean(tf * tf, axis=-1, keepdims=True) + RMS_EPS)
    return (tf * gain.astype(jnp.float32)).astype(t.dtype)


def _split(t, widths):
    cuts = [int(c) for c in np.cumsum(widths)[:-1]]
    return jnp.split(t, cuts, axis=-1)


def _split_heads(t, n_heads):
    b, s, _ = t.shape
    return t.reshape(b, s, n_heads, -1).transpose(0, 2, 1, 3)


def _merge_heads(t):
    b, h, s, d = t.shape
    return t.transpose(0, 2, 1, 3).reshape(b, s, h * d)


def _to_blocks(t, blk):
    b, h, s = t.shape[:3]
    t = t.reshape((b, h, s // blk, blk) + t.shape[3:])
    return jnp.moveaxis(t, 2, 0)


def _from_blocks(o):
    nb, b, h, blk, d = o.shape
    return jnp.moveaxis(o, 0, 2).reshape(b, h, nb * blk, d)


def _causal_softmax_attention(q, k, v, log_decay=None):
    s_len = q.shape[2]
    scale = q.shape[-1] ** -0.5
    kpos = jnp.arange(s_len, dtype=jnp.int32)
    starts = jnp.arange(s_len // Q_BLOCK, dtype=jnp.int32) * Q_BLOCK
    xs = (_to_blocks(q, Q_BLOCK), starts)
    if log_decay is not None:
        xs = xs + (_to_blocks(log_decay, Q_BLOCK),)

    def step(args):
        qb, start = args[0], args[1]
        logits = jnp.einsum("bhqd,bhkd->bhqk", qb, k).astype(jnp.float32) * scale
        if log_decay is not None:
            logits = logits + args[2][..., None] - log_decay[:, :, None, :]
        qpos = start + jnp.arange(Q_BLOCK, dtype=jnp.int32)
        causal = kpos[None, :] <= qpos[:, None]
        p = jax.nn.softmax(jnp.where(causal, logits, -jnp.inf), axis=-1)
        return jnp.einsum("bhqk,bhkd->bhqd", p.astype(v.dtype), v)

    return _from_blocks(lax.map(step, xs))


def _stick_breaking_attention(q, k, v):
    s_len = q.shape[2]
    scale = q.shape[-1] ** -0.5
    kpos = jnp.arange(s_len, dtype=jnp.int32)
    starts = jnp.arange(s_len // Q_BLOCK, dtype=jnp.int32) * Q_BLOCK

    def step(args):
        qb, start = args
        z = jnp.einsum("bhqd,bhkd->bhqk", qb, k).astype(jnp.float32) * scale
        qpos = start + jnp.arange(Q_BLOCK, dtype=jnp.int32)
        strict = kpos[None, :] < qpos[:, None]
        log_1m = jnp.where(strict, jax.nn.log_sigmoid(-z), 0.0)
        between = lax.cumsum(log_1m, axis=3, reverse=True) - log_1m
        log_w = jnp.where(strict, jax.nn.log_sigmoid(z) + between, -jnp.inf)
        return jnp.einsum("bhqk,bhkd->bhqd", jnp.exp(log_w).astype(v.dtype), v)

    return _from_blocks(lax.map(step, (_to_blocks(q, Q_BLOCK), starts)))


def _t5_bucket(rel):
    n = jnp.maximum(rel, 0)
    nf = jnp.maximum(n, 1).astype(jnp.float32)
    large = REL_MAX_EXACT + (jnp.log(nf / REL_MAX_EXACT)
                             / math.log(REL_MAX_DISTANCE / REL_MAX_EXACT)
                             * (REL_BUCKETS - REL_MAX_EXACT)).astype(jnp.int32)
    large = jnp.minimum(large, REL_BUCKETS - 1)
    return jnp.where(n < REL_MAX_EXACT, n, large)


def _moba_attention(q, k, v, rel_bias):
    b, h, s_len, d = q.shape
    scale = d ** -0.5
    s_pad = -(-s_len // MOBA_BLOCK) * MOBA_BLOCK
    pad = ((0, 0), (0, 0), (0, s_pad - s_len), (0, 0))
    q, k, v = (jnp.pad(t, pad) for t in (q, k, v))
    n_kb = s_pad // MOBA_BLOCK
    top = min(MOBA_TOPK, n_kb)
    kb = k.reshape(b, h, n_kb, MOBA_BLOCK, d)
    vb = v.reshape(b, h, n_kb, MOBA_BLOCK, d)
    k_mean = jnp.mean(kb.astype(jnp.float32), axis=3).astype(q.dtype)
    bi = jnp.arange(b)[:, None, None, None]
    hi = jnp.arange(h)[None, :, None, None]
    offs = jnp.arange(MOBA_BLOCK, dtype=jnp.int32)
    blk_ids = jnp.arange(n_kb, dtype=jnp.int32)
    starts = jnp.arange(s_pad // MOBA_Q_CHUNK, dtype=jnp.int32) * MOBA_Q_CHUNK
    n_sel = top * MOBA_BLOCK

    def step(args):
        qc, start = args
        cur = start // MOBA_BLOCK
        qpos = start + jnp.arange(MOBA_Q_CHUNK, dtype=jnp.int32)
        gate = jnp.einsum("bhqd,bhnd->bhqn", qc, k_mean).astype(jnp.float32)
        gate = jnp.where(blk_ids < cur, gate, -jnp.inf)
        _, sel = lax.top_k(gate, top)
        valid = sel < cur
        k_sel = kb[bi, hi, sel]
        v_sel = vb[bi, hi, sel]
        kpos_sel = sel[..., None] * MOBA_BLOCK + offs
        bias_sel = rel_bias[hi[..., None], _t5_bucket(qpos[:, None, None] - kpos_sel)]
        s_sel = (jnp.einsum("bhqd,bhqrkd->bhqrk", qc, k_sel).astype(jnp.float32) * scale
                 + bias_sel.astype(jnp.float32))
        s_sel = jnp.where(valid[..., None], s_sel, -jnp.inf)
        k_own = lax.dynamic_index_in_dim(kb, cur, axis=2, keepdims=False)
        v_own = lax.dynamic_index_in_dim(vb, cur, axis=2, keepdims=False)
        kpos_own = cur * MOBA_BLOCK + offs
        bias_own = rel_bias[:, _t5_bucket(qpos[:, None] - kpos_own[None, :])]
        s_own = (jnp.einsum("bhqd,bhkd->bhqk", qc, k_own).astype(jnp.float32) * scale
                 + bias_own.astype(jnp.float32))
        s_own = jnp.where(kpos_own[None, :] <= qpos[:, None], s_own, -jnp.inf)
        logits = jnp.concatenate([s_sel.reshape(b, h, MOBA_Q_CHUNK, n_sel), s_own], axis=-1)
        p = jax.nn.softmax(logits, axis=-1).astype(v.dtype)
        p_sel = p[..., :n_sel].reshape(b, h, MOBA_Q_CHUNK, top, MOBA_BLOCK)
        p_own = p[..., n_sel:]
        return (jnp.einsum("bhqrk,bhqrkd->bhqd", p_sel, v_sel)
                + jnp.einsum("bhqk,bhkd->bhqd", p_own, v_own))

    out = _from_blocks(lax.map(step, (_to_blocks(q, MOBA_Q_CHUNK), starts)))
    return out[:, :, :s_len]


def _rope_tail(t):
    s_len = t.shape[2]
    half = MLA_ROPE_DIM // 2
    inv_freq = ROPE_BASE ** (-jnp.arange(half, dtype=jnp.float32) / half)
    ang = jnp.arange(s_len, dtype=jnp.float32)[:, None] * inv_freq[None, :]
    cos, sin = jnp.cos(ang), jnp.sin(ang)
    nope, x1, x2 = _split(t, (MLA_NOPE_DIM, half, half))
    x1f, x2f = x1.astype(jnp.float32), x2.astype(jnp.float32)
    rot = jnp.concatenate([x1f * cos - x2f * sin, x1f * sin + x2f * cos], axis=-1)
    return jnp.concatenate([nope, rot.astype(t.dtype)], axis=-1)


def _even_mixer(h, w_in, forget_bias, fox_q_norm, fox_k_norm):
    qa, ka, va, fa, qb, kb, vb = _split(h @ w_in, EVEN_WIDTHS)
    qa = _rms_norm(_split_heads(qa, N_FOX_HEADS), fox_q_norm)
    ka = _rms_norm(_split_heads(ka, N_FOX_HEADS), fox_k_norm)
    log_f = jax.nn.log_sigmoid(fa.astype(jnp.float32) + forget_bias.astype(jnp.float32))
    log_decay = lax.cumsum(log_f.transpose(0, 2, 1), axis=2)
    out_a = _causal_softmax_attention(qa, ka, _split_heads(va, N_FOX_HEADS), log_decay)
    out_b = _stick_breaking_attention(_split_heads(qb, N_SB_HEADS), _split_heads(kb, N_SB_HEADS),
                                      _split_heads(vb, N_SB_HEADS))
    return jnp.concatenate([_merge_heads(out_a), _merge_heads(out_b)], axis=-1)


def _odd_mixer(h, w_in, moba_q_norm, moba_k_norm, q_a_norm, w_q_b, kv_a_norm, w_kv_b,
               mla_q_norm, mla_k_norm, rel_bias):
    qc, kc, vc, q_lat, kv_lat, k_rope = _split(h @ w_in, ODD_WIDTHS)
    qc = _rms_norm(_split_heads(qc, N_MOBA_HEADS), moba_q_norm)
    kc = _rms_norm(_split_heads(kc, N_MOBA_HEADS), moba_k_norm)
    out_c = _moba_attention(qc, kc, _split_heads(vc, N_MOBA_HEADS), rel_bias)
    qd = _split_heads(_rms_norm(q_lat, q_a_norm) @ w_q_b, N_MLA_HEADS)
    kvd = _split_heads(_rms_norm(kv_lat, kv_a_norm) @ w_kv_b, N_MLA_HEADS)
    k_nope, vd = _split(kvd, (MLA_NOPE_DIM, MLA_V_DIM))
    b, _, s_len, _ = k_nope.shape
    k_r = jnp.broadcast_to(k_rope[:, None], (b, N_MLA_HEADS, s_len, MLA_ROPE_DIM))
    kd = jnp.concatenate([k_nope, k_r], axis=-1)
    qd = _rope_tail(_rms_norm(qd, mla_q_norm))
    kd = _rope_tail(_rms_norm(kd, mla_k_norm))
    out_d = _causal_softmax_attention(qd, kd, vd)
    return jnp.concatenate([_merge_heads(out_c), _merge_heads(out_d)], axis=-1)


def _swiglu(h, w_gate_up, w_down):
    g, u = jnp.split(h @ w_gate_up, 2, axis=-1)
    return (jax.nn.silu(g) * u) @ w_down


def setup_inputs(seed: int = 0) -> dict:
    key = jax.random.key(seed)
    ks = jax.random.split(key, 22)

    def nrm(k, shape, scale):
        return scale * jax.random.normal(k, shape, jnp.float32)

    def gain(k, shape):
        return 1.0 + 0.02 * jax.random.normal(k, shape, jnp.float32)

    return {
        "x": jax.random.normal(ks[0], (BATCH, SEQ, D_MODEL), jnp.float32),
        "ffn_norm": gain(ks[1], (DEPTH, D_MODEL)),
        "ffn_w_gate_up": nrm(ks[2], (DEPTH, D_MODEL, 2 * D_FF), D_MODEL ** -0.5),
        "ffn_w_down": nrm(ks[3], (DEPTH, D_FF, D_MODEL), D_FF ** -0.5),
        "rel_bias": nrm(ks[4], (N_MOBA_HEADS, REL_BUCKETS), 0.5),
        "ev_norm": gain(ks[5], (N_EVEN, D_MODEL)),
        "ev_w_in": nrm(ks[6], (N_EVEN, D_MODEL, EVEN_IN), D_MODEL ** -0.5),
        "ev_forget_bias": jax.random.uniform(ks[7], (N_EVEN, N_FOX_HEADS), jnp.float32, 1.0, 4.0),
        "ev_fox_q_norm": gain(ks[8], (N_EVEN, HEAD_DIM)),
        "ev_fox_k_norm": gain(ks[9], (N_EVEN, HEAD_DIM)),
        "ev_w_out": nrm(ks[10], (N_EVEN, EVEN_MIX, D_MODEL), EVEN_MIX ** -0.5),
        "od_norm": gain(ks[11], (N_ODD, D_MODEL)),
        "od_w_in": nrm(ks[12], (N_ODD, D_MODEL, ODD_IN), D_MODEL ** -0.5),
        "od_moba_q_norm": gain(ks[13], (N_ODD, HEAD_DIM)),
        "od_moba_k_norm": gain(ks[14], (N_ODD, HEAD_DIM)),
        "od_mla_q_a_norm": gain(ks[15], (N_ODD, MLA_Q_LORA)),
        "od_mla_w_q_b": nrm(ks[16], (N_ODD, MLA_Q_LORA, N_MLA_HEADS * MLA_QK_DIM), MLA_Q_LORA ** -0.5),
        "od_mla_kv_a_norm": gain(ks[17], (N_ODD, MLA_KV_LORA)),
        "od_mla_w_kv_b": nrm(ks[18], (N_ODD, MLA_KV_LORA, N_MLA_HEADS * (MLA_NOPE_DIM + MLA_V_DIM)),
                             MLA_KV_LORA ** -0.5),
        "od_mla_q_norm": gain(ks[19], (N_ODD, MLA_QK_DIM)),
        "od_mla_k_norm": gain(ks[20], (N_ODD, MLA_QK_DIM)),
        "od_w_out": nrm(ks[21], (N_ODD, ODD_MIX, D_MODEL), ODD_MIX ** -0.5),
    }


def reference(x, ffn_norm, ffn_w_gate_up, ffn_w_down, rel_bias, ev_norm, ev_w_in,
              ev_forget_bias, ev_fox_q_norm, ev_fox_k_norm, ev_w_out, od_norm, od_w_in,
              od_moba_q_norm, od_moba_k_norm, od_mla_q_a_norm, od_mla_w_q_b,
              od_mla_kv_a_norm, od_mla_w_kv_b, od_mla_q_norm, od_mla_k_norm, od_w_out):
    for layer in range(DEPTH):
        i = layer // 2
        if layer % 2 == 0:
            h = _rms_norm(x, ev_norm[i])
            mixed = _even_mixer(h, ev_w_in[i], ev_forget_bias[i], ev_fox_q_norm[i], ev_fox_k_norm[i])
            x = x + mixed @ ev_w_out[i]
        else:
            h = _rms_norm(x, od_norm[i])
            mixed = _odd_mixer(h, od_w_in[i], od_moba_q_norm[i], od_moba_k_norm[i],
                               od_mla_q_a_norm[i], od_mla_w_q_b[i], od_mla_kv_a_norm[i],
                               od_mla_w_kv_b[i], od_mla_q_norm[i], od_mla_k_norm[i], rel_bias)
            x = x + mixed @ od_w_out[i]
        h = _rms_norm(x, ffn_norm[layer])
        x = x + _swiglu(h, ffn_w_gate_up[layer], ffn_w_down[layer])
    return x
```

```python
import contextlib
import math
import types
import numpy as np
import concourse.bass as bass
import concourse.mybir as mybir
from concourse.bass_utils import run_bass_kernel_spmd

F32 = mybir.dt.float32
BF16 = mybir.dt.bfloat16
ALU = mybir.AluOpType
AF = mybir.ActivationFunctionType
AX = mybir.AxisListType

S = 4096
DM = 1024
DFF = 2816
NF = DFF // 128
NT = 8
EVEN_IN = 3080
ODD_IN = 1952
EPS = 1e-6
BIGNEG = 32768.0
GLEN = 1151

ENGS = ("pe", "act", "dve", "pool", "sp")
DMA_SEMS = {"sp": 12, "pool": 16, "act": 2}
SEM_EPOCH = 20000


class Res:
    __slots__ = ("name", "last_w", "readers")

    def __init__(self, name):
        self.name = name
        self.last_w = None
        self.readers = []


class Op:
    __slots__ = ("eng", "fn", "deps", "is_dma", "sig", "has_dep", "idx")

    def __init__(self, eng, fn, is_dma):
        self.eng = eng
        self.fn = fn
        self.deps = set()
        self.is_dma = is_dma
        self.sig = None
        self.has_dep = False
        self.idx = -1


def _freeze(fn):
    if fn.__closure__ is None:
        return fn
    cells = []
    for c in fn.__closure__:
        try:
            cells.append(types.CellType(c.cell_contents))
        except ValueError:
            cells.append(c)
    return types.FunctionType(fn.__code__, fn.__globals__, fn.__name__, fn.__defaults__, tuple(cells))


class Prog:
    def __init__(self, nc):
        self.nc = nc
        self.ops = []

    def res(self, name="r"):
        return Res(name)

    def op(self, eng, fn, reads=(), writes=(), is_dma=False):
        o = Op(eng, _freeze(fn), is_dma)
        o.idx = len(self.ops)
        for r in reads:
            if r.last_w is not None:
                o.deps.add(r.last_w)
        for r in writes:
            if r.last_w is not None:
                o.deps.add(r.last_w)
            for rd in r.readers:
                o.deps.add(rd)
        o.deps.discard(o.idx)
        for r in reads:
            r.readers.append(o.idx)
        for r in writes:
            r.last_w = o.idx
            r.readers = []
        self.ops.append(o)
        return o.idx

    def dma(self, eng, out, in_, reads=(), writes=()):
        return self.op(eng, lambda e: e.dma_start(out=out, in_=in_), reads, writes, is_dma=True)

    def finalize_and_emit(self, out_ops=()):
        nc = self.nc
        ops = self.ops
        dma_last = {}
        dma_val = {}
        nd = {q: 0 for q in DMA_SEMS}
        for o in ops:
            if o.is_dma:
                j = (o.eng, nd[o.eng] % DMA_SEMS[o.eng])
                nd[o.eng] += 1
                if dma_last.get(j) is not None:
                    o.deps.add(dma_last[j])
                dma_last[j] = o.idx
                dma_val[j] = dma_val.get(j, 0) + 16
                o.sig = ("dma", j, dma_val[j])
        for o in ops:
            for d in o.deps:
                ops[d].has_dep = True
        for i in out_ops:
            ops[i].has_dep = True
        cnt = {e: 0 for e in ENGS}
        for o in ops:
            if not o.is_dma and o.has_dep:
                cnt[o.eng] += 1
                ep, v = divmod(cnt[o.eng] - 1, SEM_EPOCH)
                o.sig = (o.eng, ep, v + 1)
        n_ep = {e: max(1, (cnt[e] + SEM_EPOCH - 1) // SEM_EPOCH) for e in ENGS}
        with contextlib.ExitStack() as st:
            sems = {}
            for e in ENGS:
                for ep in range(n_ep[e]):
                    sems[(e, ep)] = st.enter_context(nc.semaphore(f"s_{e}_{ep}"))
            for q, n in DMA_SEMS.items():
                for j in range(n):
                    sems[("dma", (q, j))] = st.enter_context(nc.semaphore(f"s_dma_{q}_{j}"))
            block = st.enter_context(nc.Block())
            by_eng = {e: [o for o in ops if o.eng == e] for e in ENGS}

            def make(ename):
                def body(eng):
                    waited = {}
                    for o in by_eng[ename]:
                        need = {}
                        for d in o.deps:
                            od = ops[d]
                            if od.eng == "pe" and ename == "pe" and not od.is_dma and not o.is_dma:
                                continue
                            k = od.sig[:2]
                            v = od.sig[2]
                            if waited.get(k, 0) >= v:
                                continue
                            if need.get(k, 0) < v:
                                need[k] = v
                        for k, v in need.items():
                            eng.wait_ge(sems[k], v)
                            waited[k] = v
                        ins = o.fn(eng)
                        if o.is_dma:
                            ins.then_inc(sems[o.sig[:2]], 16)
                        elif o.has_dep:
                            ins.then_inc(sems[o.sig[:2]], 1)
                    if ename == "sp":
                        for i in out_ops:
                            od = ops[i]
                            eng.wait_ge(sems[od.sig[:2]], od.sig[2])
                return body

            block.tensor(make("pe"))
            block.scalar(make("act"))
            block.vector(make("dve"))
            block.gpsimd(make("pool"))
            block.sync(make("sp"))
        return cnt


def _t5_bucket_np(rel):
    n = np.maximum(rel, 0).astype(np.int32)
    nf = np.maximum(n, 1).astype(np.float32)
    large = 16 + (np.log(nf / np.float32(16)) / np.float32(math.log(128 / 16)) * np.float32(16)).astype(np.int32)
    large = np.minimum(large, 31)
    return np.where(n < 16, n, large)


def host_consts():
    c = {}
    c["ident"] = np.eye(128, dtype=np.float32)
    p = np.arange(128)[:, None]
    j = np.arange(512)[None, :]
    mle = np.zeros((128, 4, 512), np.float32)
    mlt = np.zeros((128, 4, 512), np.float32)
    cum = np.zeros((128, 4, 512), np.float32)
    for r in range(4):
        mle[:, r, :] = np.where(j >= p + 128 * r, 0.0, -30000.0)
        mlt[:, r, :] = (j > p + 128 * r)
        cum[:, r, :] = (p + 128 * r <= j)
    c["mask_le"] = mle
    c["mask_lt"] = mlt
    c["cum"] = cum
    jj = np.arange(128)[:, None]
    kk = np.arange(128)[None, :]
    c["triinc"] = (jj >= kk).astype(np.float32)
    c["tristr"] = (jj < kk).astype(np.float32)
    half = 16
    inv_freq = (np.float32(10000.0) ** (-np.arange(half, dtype=np.float32) / np.float32(half))).astype(np.float32)
    ang = np.arange(S, dtype=np.float32)[:, None] * inv_freq[None, :]
    cos = np.cos(ang).astype(np.float32).T
    sin = np.sin(ang).astype(np.float32).T
    rc = np.ones((96, S), np.float32)
    rs = np.zeros((96, S), np.float32)
    rc[64:80] = cos
    rc[80:96] = cos
    rs[64:80] = sin
    rs[80:96] = sin
    c["rope_c"] = rc
    c["rope_s"] = rs
    perm = np.zeros((96, 96), np.float32)
    for i in range(16):
        perm[80 + i, 64 + i] = -1.0
        perm[64 + i, 80 + i] = 1.0
    c["perm"] = perm
    v = np.arange(GLEN)
    rel = 639 - v
    bk = _t5_bucket_np(rel)
    oh = np.zeros((32, GLEN), np.float32)
    for b in range(32):
        oh[b] = ((bk == b) & (rel >= 0))
    c["ohr"] = oh
    c["stepmask"] = np.broadcast_to((rel >= 0).astype(np.float32)[None, :], (8, GLEN)).copy()
    s = np.arange(S)[None, :]
    n = np.arange(16)[:, None]
    c["kblockhot"] = (BIGNEG * ((s // 256) == n)).astype(np.float32)
    return c


CONST_SHAPES = {
    "ident": [128, 128], "mask_le": [128, 4, 512], "mask_lt": [128, 4, 512], "cum": [128, 4, 512],
    "triinc": [128, 128], "tristr": [128, 128], "rope_c": [96, S], "rope_s": [96, S], "perm": [96, 96],
    "ohr": [32, GLEN], "stepmask": [8, GLEN], "kblockhot": [16, S],
}

NCOLS = 32


def host_cols(inp, layer):
    t = np.zeros((128, NCOLS), np.float32)
    i = layer // 2
    fm = lambda v: np.asarray(v, np.float32).reshape(-1, 128).T
    t[:, 8:16] = fm(inp["ffn_norm"][layer])
    if layer % 2 == 0:
        t[:, 0:8] = fm(inp["ev_norm"][i])
        t[:, 16] = np.tile(np.asarray(inp["ev_fox_q_norm"][i], np.float32), 2)
        t[:, 17] = np.tile(np.asarray(inp["ev_fox_k_norm"][i], np.float32), 2)
        t[:, 24:32] = np.broadcast_to(np.asarray(inp["ev_forget_bias"][i], np.float32)[None, :], (128, 8))
    else:
        t[:, 0:8] = fm(inp["od_norm"][i])
        t[:, 16] = np.tile(np.asarray(inp["od_moba_q_norm"][i], np.float32), 2)
        t[:, 17] = np.tile(np.asarray(inp["od_moba_k_norm"][i], np.float32), 2)
        t[:, 18:20] = fm(inp["od_mla_q_a_norm"][i])
        t[:, 20] = np.asarray(inp["od_mla_kv_a_norm"][i], np.float32)
        t[0:96, 21] = np.asarray(inp["od_mla_q_norm"][i], np.float32)
        t[0:96, 22] = np.asarray(inp["od_mla_k_norm"][i], np.float32)
        t[:, 24:32] = np.broadcast_to(np.asarray(inp["rel_bias"], np.float32)[:, 31][None, :], (128, 8))
    return t


def build(n_layers=4, dbg=False, skip_att=(), skip_kinds=(), parts="kqsvlQKVga"):
    nc = bass.Bass("TRN2", target_bir_lowering=False)
    P = Prog(nc)
    st = contextlib.ExitStack()

    def din(name, shape):
        return nc.dram_tensor(name, list(shape), F32, kind="ExternalInput").ap()

    def dscr(name, shape, dt):
        kind = "ExternalOutput" if (dbg and name in ("QT", "KT", "VS", "MIX", "XT", "GSR")) else "Internal"
        return nc.dram_tensor(name, list(shape), dt, kind=kind).ap()

    def dap(base, offset, ap):
        return bass.AP(tensor=base.tensor, offset=base.offset + offset, ap=[list(a) for a in ap])

    x_in = din("x", [S, DM])
    out_d = nc.dram_tensor("out", [S, DM], F32, kind="ExternalOutput").ap()
    W = {}
    W["ffn_gu"] = din("ffn_w_gate_up", [4, DM, 2 * DFF])
    W["ffn_d"] = din("ffn_w_down", [4, DFF, DM])
    W["ev_in"] = din("ev_w_in", [2, DM, EVEN_IN])
    W["ev_out"] = din("ev_w_out", [2, DM, DM])
    W["od_in"] = din("od_w_in", [2, DM, ODD_IN])
    W["od_out"] = din("od_w_out", [2, DM, DM])
    W["qb"] = din("od_mla_w_q_b", [2, 256, 768])
    W["kvb"] = din("od_mla_w_kv_b", [2, 128, 1024])
    cols_d = din("cols", [4, 128, NCOLS])
    rbT_d = din("rbT", [32, 8])
    C = {k: din("c_" + k, shp) for k, shp in CONST_SHAPES.items()}

    XT = dscr("XT", [128, 8, S], F32)
    QT = dscr("QT", [16, 96, S], BF16)
    KT = dscr("KT", [16, 96, S], BF16)
    VS = dscr("VS", [S, DM], BF16)
    MIX = dscr("MIX", [128, 8, S], BF16)
    GSR = dscr("GSR", [8, GLEN], BF16)
    class CW:
        def __init__(self, name, src2d, nk, chunks):
            self.src, self.nk, self.chunks = src2d, nk, chunks
            self.off = []
            tot = 0
            for (c0, n) in chunks:
                self.off.append(tot)
                tot += 128 * nk * n
            self.d = dscr(name, [tot], BF16)
            self.r = P.res(name)

        def convert(self):
            nk = self.nk
            ci = 0
            while ci < len(self.chunks):
                c0, n = self.chunks[ci]
                cj = ci
                while (cj + 1 < len(self.chunks) and self.chunks[cj + 1][1] == n
                       and self.chunks[cj + 1][0] == self.chunks[cj][0] + n):
                    cj += 1
                nch = cj - ci + 1
                for kc in range(nk):
                    dst = dap(self.d, self.off[ci] + kc * n, [[nk * n, 128], [128 * nk * n, nch], [1, n]])
                    src = self.src[kc * 128:(kc + 1) * 128, c0:c0 + nch * n].rearrange("p (c n) -> p c n", n=n)
                    P.dma("pool", dst, src, writes=[self.r])
                ci = cj + 1

        def load(self, ci):
            c0, n = self.chunks[ci]
            nk = self.nk
            src = dap(self.d, self.off[ci], [[nk * n, 128], [1, nk * n]])
            sl = wslot_load([(lambda t: t[:, 0:nk * n], src)], [self.r])
            return sl, v3(sl.t, nk, n)

    EV_CH = [(0, 512), (512, 512), (1024, 512), (1536, 8), (1544, 512), (2056, 512), (2568, 512)]
    OD_CH = [(0, 512), (512, 512), (1024, 512), (1536, 416)]
    FGRP = [(0, 4), (4, 8), (8, 12), (12, 16), (16, 20), (20, 22)]
    GU_CH = [(f0 * 128, (f1 - f0) * 128) for f0, f1 in FGRP] + [(DFF + f0 * 128, (f1 - f0) * 128) for f0, f1 in FGRP]
    DGRP = [(0, 8), (8, 16), (16, 22)]
    WIN, WOUT, WGU, WD = [], [], [], []
    for l in range(4):
        i = l // 2
        WIN.append(CW(f"WIN{l}", (W["ev_in"] if l % 2 == 0 else W["od_in"])[i], 8, EV_CH if l % 2 == 0 else OD_CH))
        WOUT.append(CW(f"WOUT{l}", (W["ev_out"] if l % 2 == 0 else W["od_out"])[i], 8, [(0, 512), (512, 512)]))
        WGU.append(CW(f"WGU{l}", W["ffn_gu"][l], 8, GU_CH))
        WD.append([CW(f"WD{l}_{g}", W["ffn_d"][l][f0 * 128:f1 * 128, :], f1 - f0, [(0, 512), (512, 512)])
                   for g, (f0, f1) in enumerate(DGRP)])
    WQB = [dscr(f"WQB{i}", [128, 2, 768], BF16) for i in range(2)]
    WKVB = [dscr(f"WKVB{i}", [128, 1024], BF16) for i in range(2)]
    r_XT = [P.res() for _ in range(NT)]
    r_QT = [[P.res() for _ in range(NT)] for _ in range(16)]
    r_KT = [[P.res() for _ in range(NT)] for _ in range(16)]
    r_KTaug = [P.res() for _ in range(16)]
    r_VS = [[P.res() for _ in range(NT)] for _ in range(2)]
    r_MIX = [P.res() for _ in range(NT)]
    r_GSR = P.res()
    r_W = {}

    def sb(name, shape, dt):
        t = st.enter_context(nc.sbuf_tensor("sb_" + name, list(shape), dt))
        return t

    ARENA_BYTES = 146 * 1024
    arena = sb("arena", [128, ARENA_BYTES // 2], BF16)
    aoff = {"R": 0, "A": 0}
    phase_res = {"R": [], "A": []}

    class T:
        def __init__(self, name, shape, dt, phase=None):
            self.r = P.res(name)
            if phase is None:
                self.t = sb(name, shape, dt)
                return
            esz = 4 if dt == F32 else 2
            n = 1
            for d in shape[1:]:
                n *= d
            nbytes = (n * esz + 63) // 64 * 64
            o = aoff[phase]
            aoff[phase] = o + nbytes
            assert aoff[phase] <= ARENA_BYTES, (name, phase, aoff[phase])
            v = arena[0:shape[0], o // 2:o // 2 + n * esz // 2]
            if dt == F32:
                v = v.bitcast(F32)
            if len(shape) == 3:
                v = v.rearrange("p (a b) -> p a b", b=shape[2])
            self.t = v
            phase_res[phase].append(self.r)

    barscr = sb("barscr", [128, 8], F32)

    def phase_barrier():
        P.op("pool", lambda e: e.memset(barscr[:], 0.0), [], phase_res["R"] + phase_res["A"])

    PSBIG = st.enter_context(nc.psum_tensor("psbig", [128, 4096], F32))
    PS = [PSBIG[:, i * 512:(i + 1) * 512] for i in range(8)]
    r_PS = [P.res(f"ps{i}") for i in range(8)]

    ident = T("ident", [128, 128], F32)
    ones_bf = T("ones_bf", [128, 128], BF16)
    ident_bf = T("ident_bf", [128, 128], BF16)
    blk64 = T("blk64", [128, 128], BF16)
    mask_le = T("mask_le", [128, 4, 512], BF16)
    mask_lt = T("mask_lt", [128, 4, 512], BF16)
    cumc = T("cumc", [128, 4, 512], F32)
    triinc = T("triinc", [128, 128], BF16)
    tristr = T("tristr", [128, 128], BF16)
    perm = T("perm", [96, 96], F32)
    colt = [T(f"cols{l}", [128, NCOLS], F32) for l in range(4)]
    epsc = T("epsc", [128, 4], F32)
    zero128 = T("zero128", [128, 128], F32)
    onesrow = T("onesrow", [8, 512], BF16)
    negrow = T("negrow", [8, 512], BF16)

    def cdma(tile, src):
        P.dma("pool", tile.t[:], src, writes=[tile.r])

    cdma(ident, C["ident"])
    cdma(mask_le, C["mask_le"])
    cdma(mask_lt, C["mask_lt"])
    cdma(cumc, C["cum"])
    cdma(triinc, C["triinc"])
    cdma(tristr, C["tristr"])
    cdma(perm, C["perm"])
    for l in range(n_layers):
        P.dma("pool", colt[l].t[:], cols_d[l], writes=[colt[l].r])
    P.op("dve", lambda e: e.memset(ones_bf.t[:], 1.0), [], [ones_bf.r])
    P.op("dve", lambda e: e.tensor_copy(out=ident_bf.t[:], in_=ident.t[:]), [ident.r], [ident_bf.r])
    P.op("dve", lambda e: e.memset(blk64.t[:], 0.0), [], [blk64.r])
    P.op("dve", lambda e: e.memset(blk64.t[0:64, 0:64], 1.0), [], [blk64.r])
    P.op("dve", lambda e: e.memset(blk64.t[64:128, 64:128], 1.0), [], [blk64.r])
    P.op("dve", lambda e: e.memset(epsc.t[:, 0:1], EPS), [], [epsc.r])
    P.op("dve", lambda e: e.memset(epsc.t[:, 1:2], 64 * EPS), [], [epsc.r])
    P.op("dve", lambda e: e.memset(epsc.t[:, 2:3], 96 * EPS), [], [epsc.r])
    P.op("dve", lambda e: e.memset(epsc.t[:, 3:4], 1.0), [], [epsc.r])
    P.op("dve", lambda e: e.memset(zero128.t[:], 0.0), [], [zero128.r])
    P.op("dve", lambda e: e.memset(onesrow.t[:], 1.0), [], [onesrow.r])
    P.op("dve", lambda e: e.memset(negrow.t[:], -1.0), [], [negrow.r])

    def conv2d(dst, src, nk, key):
        r = P.res(key)
        r_W[key] = r
        for kc in range(nk):
            P.dma("pool", dst[:, kc, :], src[kc * 128:(kc + 1) * 128, :], writes=[r])

    def conv_layer(l):
        i = l // 2
        WIN[l].convert()
        if l % 2 == 1:
            conv2d(WQB[i], W["qb"][i], 2, f"wqb{i}")
            r = P.res()
            r_W[f"wkvb{i}"] = r
            P.dma("pool", WKVB[i], W["kvb"][i], writes=[r])

    def conv_layer_post(l):
        WOUT[l].convert()
        WGU[l].convert()
        for cw in WD[l]:
            cw.convert()

    NSLOT = 5
    wslots = [T(f"wslot{i}", [128, 4096], BF16) for i in range(NSLOT)]
    wctr = [0]

    def wload(parts):
        sl = wslots[wctr[0] % NSLOT]
        wctr[0] += 1
        return sl

    xt = [T(f"xt{i}", [128, 8, 512], F32, "R") for i in range(2)]
    hT = T("hT", [128, 8, 512], BF16, "R")
    hT.rc = [P.res(f"hT{c}") for c in range(8)]
    phase_res["R"].extend(hT.rc)
    mixTs = [T(f"mixT{i}", [128, 8, 512], BF16, "R") for i in range(2)]
    actT = T("actT", [128, NF, 512], BF16, "R")
    sqb = [T(f"sqb{i}", [128, 512], BF16, "R") for i in range(2)]
    t1 = [T(f"t1_{i}", [128, 512], F32, "R") for i in range(2)]
    rstd = [T(f"rstd{i}", [128, 512], F32, "R") for i in range(2)]
    obf = [T(f"obf{i}", [128, 512], BF16, "R") for i in range(3)]
    sg = [T(f"sg{i}", [128, 512], F32, "R") for i in range(2)]
    tok = [T(f"tok{i}", [128, 1024], F32, "R") for i in range(2)]
    fx = T("fx", [128, 8], F32, "R")
    nlf = T("nlf", [128, 4, 8], F32, "R")
    carry = T("carry", [8, 1], F32, "R")
    cT = T("cT", [8, 512], F32, "R")
    csp = [T(f"csp{i}", [8, 512], BF16, "R") for i in range(3)]
    crem = [T(f"crem{i}", [8, 512], F32, "R") for i in range(2)]
    qf = [T(f"qf{i}", [128, 512], F32, "R") for i in range(2)]
    KM = T("KM", [128, 4, 16], F32, "R")
    gsb = T("gsb", [128, 8, 16], F32, "R")
    mx8 = T("mx8", [128, 8, 8], F32, "R")
    selt = T("selt", [128, 8, 16], F32, "R")
    selm1 = T("selm1", [128, 8, 16], F32, "R")
    seltT = T("seltT", [128, 512], BF16, "R")
    qln = T("qln", [128, 2, 512], BF16, "R")
    kvn = T("kvn", [128, 512], BF16, "R")
    wqb_sb = T("wqb_sb", [128, 2, 768], BF16, "R")
    wkvb_sb = T("wkvb_sb", [128, 1024], BF16, "R")
    ropc = T("ropc", [96, 512], F32, "R")
    rops = T("rops", [96, 512], F32, "R")
    yf = [T(f"yf{i}", [96, 512], F32, "R") for i in range(2)]
    kdr = [T(f"kdraw{i}", [96, 512], F32, "R") for i in range(2)]
    sq96s = [T(f"sq96_{i}", [96, 512], BF16, "R") for i in range(2)]
    rt1 = T("rt1", [96, 512], F32, "R")
    rt2 = T("rt2", [96, 512], F32, "R")

    qsb = [T(f"qsb{i}", [96, S], BF16, "A") for i in range(2)]
    ksb = [T(f"ksb{i}", [96, S], BF16, "A") for i in range(2)]
    vaug = [T(f"vaug{i}", [128, 32, 128], BF16, "A") for i in range(2)]
    pt = [T(f"pt{i}", [128, 512], BF16, "A") for i in range(2)]
    ef = [T(f"ef{i}", [128, 512], F32, "A") for i in range(2)]
    pt2 = [T(f"pt2_{i}", [128, 1024], BF16, "A") for i in range(3)]
    ef2 = [T(f"ef2_{i}", [128, 1024], F32, "A") for i in range(2)]
    gf2 = [T(f"gf2_{i}", [128, 1024], F32, "A") for i in range(2)]
    spb2 = [T(f"spb2_{i}", [128, 1024], BF16, "A") for i in range(2)]
    rec = [T(f"rec{i}", [64, 512], F32, "A") for i in range(2)]
    ot = [T(f"ot{i}", [64, 512], BF16, "A") for i in range(2)]
    ebh = T("ebh", [128, 5, 512], BF16, "A")
    eb = [T(f"eb{i}", [128, 5, 512], BF16, "A") for i in range(2)]
    b31 = None

    def mm(bank, out_ap, lhsT, rhs, start, stop, reads):
        P.op("pe", lambda e: e.matmul(out_ap, lhsT=lhsT, rhs=rhs, start=start, stop=stop),
             reads, [r_PS[bank]])

    ps_rot = [0]

    def next_main():
        b = ps_rot[0] % 3
        ps_rot[0] += 1
        return b

    rot = {}

    def nxt(lst, key):
        i = rot.get(key, 0)
        rot[key] = i + 1
        return lst[i % len(lst)]

    def wslot_load(srcs, reads):
        sl = wslots[wctr[0] % NSLOT]
        wctr[0] += 1
        for vf, src in srcs:
            P.dma("sp", vf(sl.t), src, reads=reads, writes=[sl.r])
        return sl

    def v3(t, a, b):
        return t[:, 0:a * b].rearrange("p (a b) -> p a b", b=b)

    def norm_squares(x, chunks):
        for c in chunks:
            s = nxt(sqb, "sqb")
            P.op("act", lambda e, s=s, c=c: e.activation(out=s.t[:], in_=x.t[:, c, :], func=AF.Square),
                 [x.r], [s.r])
            mm(3, PS[3][:], ones_bf.t[:], s.t[:], c == 0, c == 7, [ones_bf.r, s.r])

    def norm_finish(x, gain_t, gcol0):
        a = nxt(t1, "t1")
        rs = nxt(rstd, "rstd")
        P.op("act", lambda e: e.activation(out=a.t[:], in_=PS[3][:], func=AF.Sqrt, bias=epsc.t[:, 0:1], scale=1.0 / DM),
             [r_PS[3], epsc.r], [a.r])
        P.op("dve", lambda e: e.reciprocal(out=rs.t[:], in_=a.t[:]), [a.r], [rs.r])
        for c in range(8):
            P.op("dve", lambda e, c=c: e.scalar_tensor_tensor(out=hT.t[:, c, :], in0=x.t[:, c, :],
                                                              scalar=gain_t.t[:, gcol0 + c:gcol0 + c + 1], in1=rs.t[:],
                                                              op0=ALU.mult, op1=ALU.mult),
                 [x.r, gain_t.r, rs.r], [hT.rc[c]])

    def norm_fm(x, gain_t, gcol0):
        norm_squares(x, range(8))
        norm_finish(x, gain_t, gcol0)

    def headnorm(bank, npart, blk_t, scale, bias_col, gain_t, gcol, out_ap, out_res, extra_f32=None):
        s = nxt(sqb, "sqb")
        P.op("act", lambda e: e.activation(out=s.t[0:npart, :], in_=PS[bank][0:npart, :], func=AF.Square),
             [r_PS[bank]], [s.r])
        mm(4, PS[4][0:npart, :], blk_t.t[0:npart, 0:npart], s.t[0:npart, :], True, True, [blk_t.r, s.r])
        a = nxt(t1, "t1")
        rs = nxt(rstd, "rstd")
        P.op("act", lambda e: e.activation(out=a.t[0:npart, :], in_=PS[4][0:npart, :], func=AF.Sqrt,
                                           bias=epsc.t[0:npart, bias_col:bias_col + 1], scale=scale),
             [r_PS[4], epsc.r], [a.r])
        P.op("dve", lambda e: e.reciprocal(out=rs.t[0:npart, :], in_=a.t[0:npart, :]), [a.r], [rs.r])
        if extra_f32 is not None:
            P.op("dve", lambda e: e.scalar_tensor_tensor(out=extra_f32.t[0:npart, :], in0=PS[bank][0:npart, :],
                                                         scalar=gain_t.t[0:npart, gcol:gcol + 1], in1=rs.t[0:npart, :],
                                                         op0=ALU.mult, op1=ALU.mult),
                 [r_PS[bank], gain_t.r, rs.r], [extra_f32.r])
            if out_ap is not None:
                P.op("pool", lambda e: e.tensor_copy(out=out_ap, in_=extra_f32.t[0:npart, :]), [extra_f32.r], [out_res])
        else:
            P.op("dve", lambda e: e.scalar_tensor_tensor(out=out_ap, in0=PS[bank][0:npart, :],
                                                         scalar=gain_t.t[0:npart, gcol:gcol + 1], in1=rs.t[0:npart, :],
                                                         op0=ALU.mult, op1=ALU.mult),
                 [r_PS[bank], gain_t.r, rs.r], [out_res])

    def proj_fm(bank, M, lhs_fn, lhs_res, rhs_t=None, nk=8):
        rhs_t = rhs_t or hT
        for kc in range(nk):
            rr = rhs_t.rc[kc] if hasattr(rhs_t, "rc") else rhs_t.r
            mm(bank, PS[bank][0:M, :], lhs_fn(kc), rhs_t.t[:, kc, :], kc == 0, kc == nk - 1, [lhs_res, rr])

    def store_heads(dst, tile, h0, tt, res_list):
        for hh in range(2):
            P.dma("pool", dst[h0 + hh, 0:64, tt * 512:(tt + 1) * 512], tile.t[hh * 64:(hh + 1) * 64, :],
                  reads=[tile.r], writes=[res_list[h0 + hh][tt]])

    def inproj_even(l, tt):
        win = WIN[l]
        ct = colt[l]
        for grp, ci in (("qa", 0), ("ka", 1), ("qb", 4), ("kb", 5)):
            sl, w3 = win.load(ci)

            def fin(ch, b, grp=grp):
                o = nxt(obf, "obf")
                if grp == "qa":
                    headnorm(b, 128, blk64, 1.0, 1, ct, 16, o.t[:], o.r)
                    store_heads(QT, o, 2 * ch, tt, r_QT)
                elif grp == "ka":
                    headnorm(b, 128, blk64, 1.0 / 64, 0, ct, 17, o.t[:], o.r)
                    store_heads(KT, o, 2 * ch, tt, r_KT)
                elif grp == "qb":
                    P.op("act", lambda e, b=b, o=o: e.activation(out=o.t[:], in_=PS[b][:], func=AF.Copy, scale=0.125),
                         [r_PS[b]], [o.r])
                    store_heads(QT, o, 8 + 2 * ch, tt, r_QT)
                else:
                    P.op("act", lambda e, b=b, o=o: e.activation(out=o.t[:], in_=PS[b][:], func=AF.Copy),
                         [r_PS[b]], [o.r])
                    store_heads(KT, o, 8 + 2 * ch, tt, r_KT)

            pend = None
            for ch in range(4):
                b = next_main()
                proj_fm(b, 128, lambda kc, ch=ch: w3[:, kc, ch * 128:(ch + 1) * 128], sl.r)
                if pend is not None:
                    fin(*pend)
                pend = (ch, b)
            fin(*pend)
        for g, ci in ((0, 2), (1, 6)):
            sl, w3 = win.load(ci)
            for sub in range(4):
                b = next_main()
                for kc in range(8):
                    mm(b, PS[b][:], hT.t[:, kc, sub * 128:(sub + 1) * 128], w3[:, kc, :], kc == 0, kc == 7, [hT.rc[kc], sl.r])
                o = nxt(obf, "obf")
                if sub % 2 == 0:
                    P.op("act", lambda e, b=b, o=o: e.activation(out=o.t[:], in_=PS[b][:], func=AF.Copy), [r_PS[b]], [o.r])
                else:
                    P.op("dve", lambda e, b=b, o=o: e.tensor_copy(out=o.t[:], in_=PS[b][:]), [r_PS[b]], [o.r])
                P.dma("pool", VS[tt * 512 + sub * 128: tt * 512 + (sub + 1) * 128, g * 512:(g + 1) * 512], o.t[:],
                      reads=[o.r], writes=[r_VS[g][tt]])
        sl, w3 = win.load(3)
        for sub in range(4):
            for kc in range(8):
                mm(5, PS[5][:, sub * 8:(sub + 1) * 8], hT.t[:, kc, sub * 128:(sub + 1) * 128], w3[:, kc, :],
                   kc == 0, kc == 7, [hT.rc[kc], sl.r])
        for sub in range(4):
            P.op("dve", lambda e, sub=sub: e.tensor_tensor(out=nlf.t[:, sub, :], in0=PS[5][:, sub * 8:(sub + 1) * 8],
                                                           in1=ct.t[:, 24:32], op=ALU.add), [r_PS[5], ct.r], [nlf.r])
        P.op("act", lambda e: e.activation(out=nlf.t[:], in_=nlf.t[:], func=AF.Exp, scale=-1.0), [nlf.r], [nlf.r])
        P.op("act", lambda e: e.activation(out=nlf.t[:], in_=nlf.t[:], func=AF.Ln, bias=epsc.t[:, 3:4], scale=1.0),
             [nlf.r, epsc.r], [nlf.r])
        for sub in range(4):
            mm(6, PS[6][0:8, :], nlf.t[:, sub, :], cumc.t[:, sub, :], sub == 0, sub == 3, [nlf.r, cumc.r])
        if tt == 0:
            P.op("dve", lambda e: e.memset(carry.t[:], 0.0), [], [carry.r])
        P.op("dve", lambda e: e.tensor_scalar(out=cT.t[:], in0=PS[6][0:8, :], scalar1=carry.t[:, 0:1], scalar2=-1.0,
                                              op0=ALU.add, op1=ALU.mult), [r_PS[6], carry.r], [cT.r])
        P.op("dve", lambda e: e.tensor_scalar(out=carry.t[:], in0=cT.t[:, 511:512], scalar1=-1.0, scalar2=None,
                                              op0=ALU.mult), [cT.r], [carry.r])
        P.op("dve", lambda e: e.tensor_copy(out=csp[0].t[:], in_=cT.t[:]), [cT.r], [csp[0].r])
        P.op("dve", lambda e: e.tensor_tensor(out=crem[0].t[:], in0=cT.t[:], in1=csp[0].t[:], op=ALU.subtract),
             [cT.r, csp[0].r], [crem[0].r])
        P.op("dve", lambda e: e.tensor_copy(out=csp[1].t[:], in_=crem[0].t[:]), [crem[0].r], [csp[1].r])
        P.op("dve", lambda e: e.tensor_tensor(out=crem[1].t[:], in0=crem[0].t[:], in1=csp[1].t[:], op=ALU.subtract),
             [crem[0].r, csp[1].r], [crem[1].r])
        P.op("dve", lambda e: e.tensor_copy(out=csp[2].t[:], in_=crem[1].t[:]), [crem[1].r], [csp[2].r])
        for k3 in range(3):
            P.dma("pool", dap(QT, (64 + k3) * S + tt * 512, [[96 * S, 8], [1, 512]]), csp[k3].t[:],
                  reads=[csp[k3].r], writes=[r_QT[h][tt] for h in range(8)])
            P.dma("pool", dap(KT, (67 + k3) * S + tt * 512, [[96 * S, 8], [1, 512]]), csp[k3].t[:],
                  reads=[csp[k3].r], writes=[r_KT[h][tt] for h in range(8)])
        for k3 in range(3):
            P.dma("pool", dap(QT, (67 + k3) * S + tt * 512, [[96 * S, 8], [1, 512]]), negrow.t[:],
                  reads=[negrow.r], writes=[r_QT[h][tt] for h in range(8)])
            P.dma("pool", dap(KT, (64 + k3) * S + tt * 512, [[96 * S, 8], [1, 512]]), onesrow.t[:],
                  reads=[onesrow.r], writes=[r_KT[h][tt] for h in range(8)])

    def inproj_odd(l, tt):
        i = l // 2
        win = WIN[l]
        ct = colt[l]
        if tt == 0:
            P.dma("sp", wqb_sb.t[:], WQB[i], reads=[r_W[f"wqb{i}"]], writes=[wqb_sb.r])
            P.dma("sp", wkvb_sb.t[:], WKVB[i], reads=[r_W[f"wkvb{i}"]], writes=[wkvb_sb.r])
            P.op("dve", lambda e: e.memset(gsb.t[:], -1e30), [], [gsb.r])
            P.op("dve", lambda e: e.memset(selm1.t[:], 0.0), [], [selm1.r])
            P.op("dve", lambda e: e.memset(KM.t[:], 0.0), [], [KM.r])
        P.dma("pool", ropc.t[:], C["rope_c"][:, tt * 512:(tt + 1) * 512], writes=[ropc.r])
        P.dma("pool", rops.t[:], C["rope_s"][:, tt * 512:(tt + 1) * 512], writes=[rops.r])
        if "k" in parts:
            sl, w3 = win.load(1)

            def fink(ch, b):
                o = nxt(obf, "obf")
                headnorm(b, 128, blk64, 1.0 / 64, 0, ct, 17, o.t[:], o.r)
                store_heads(KT, o, 2 * ch, tt, r_KT)
                P.op("dve", lambda e, o=o, ch=ch: e.tensor_reduce(out=KM.t[:, ch, 2 * tt:2 * tt + 2],
                                                                  in_=o.t[:].rearrange("p (a b) -> p a b", b=256),
                                                                  axis=AX.X, op=ALU.add), [o.r], [KM.r])

            pend = None
            for ch in range(4):
                b = next_main()
                proj_fm(b, 128, lambda kc, ch=ch: w3[:, kc, ch * 128:(ch + 1) * 128], sl.r)
                if pend is not None:
                    fink(*pend)
                pend = (ch, b)
            fink(*pend)
        if "q" in parts:
            sl, w3 = win.load(0)

            def gates(ch, qq):
                for sub in range(4):
                    if (4 * tt + sub) // 2 <= 3:
                        continue
                    for hh in range(2):
                        h = 2 * ch + hh
                        mm(5, PS[5][:, sub * 128 + h * 16: sub * 128 + (h + 1) * 16],
                           qq.t[hh * 64:(hh + 1) * 64, sub * 128:(sub + 1) * 128], KM.t[hh * 64:(hh + 1) * 64, ch, :], True, True,
                           [qq.r, KM.r])

            def finq(ch, b):
                o = nxt(obf, "obf")
                qq = nxt(qf, "qf")
                headnorm(b, 128, blk64, 1.0, 1, ct, 16, o.t[:], o.r, extra_f32=qq)
                store_heads(QT, o, 2 * ch, tt, r_QT)
                return qq

            pend = None
            gpend = None
            for ch in range(4):
                b = next_main()
                proj_fm(b, 128, lambda kc, ch=ch: w3[:, kc, ch * 128:(ch + 1) * 128], sl.r)
                if gpend is not None:
                    gates(*gpend)
                    gpend = None
                if pend is not None:
                    gpend = (pend[0], finq(*pend))
                pend = (ch, b)
            if gpend is not None:
                gates(*gpend)
            gpend = (pend[0], finq(*pend))
            gates(*gpend)
        if "s" in parts:
            for sub in range(4):
                cur = (4 * tt + sub) // 2
                trs = slice(sub * 128, (sub + 1) * 128)
                if cur <= 3:
                    P.op("pe", lambda e, trs=trs: e.transpose(PS[6][:, trs], zero128.t[:], ident.t[:]),
                         [zero128.r, ident.r], [r_PS[6]])
                    continue
                g3 = PS[5][:, trs].rearrange("p (h n) -> p h n", n=16)
                P.op("dve", lambda e, g3=g3, cur=cur: e.tensor_copy(out=gsb.t[:, :, 0:cur], in_=g3[:, :, 0:cur]),
                     [r_PS[5]], [gsb.r])
                for h in range(8):
                    P.op("dve", lambda e, h=h: e.max(out=mx8.t[:, h, :], in_=gsb.t[:, h, :]), [gsb.r], [mx8.r])
                m0 = mx8.t[:, :, 2:3]
                thr = bass.AP(tensor=m0.tensor, offset=m0.offset, ap=[list(m0.ap[0]), list(m0.ap[1]), [0, 16]])
                P.op("dve", lambda e, thr=thr: e.tensor_tensor(out=selt.t[:], in0=gsb.t[:], in1=thr, op=ALU.is_ge),
                     [gsb.r, mx8.r], [selt.r])
                P.op("dve", lambda e, cur=cur: e.tensor_scalar(out=selm1.t[:, :, 0:cur], in0=selt.t[:, :, 0:cur],
                                                               scalar1=-1.0, scalar2=None, op0=ALU.add),
                     [selt.r], [selm1.r])
                P.op("pe", lambda e, trs=trs: e.transpose(PS[6][:, trs], selm1.t[:].rearrange("p h n -> p (h n)"), ident.t[:]),
                     [selm1.r, ident.r], [r_PS[6]])
            P.op("act", lambda e: e.activation(out=seltT.t[:], in_=PS[6][:], func=AF.Copy), [r_PS[6]], [seltT.r])
            P.dma("pool", dap(QT, 64 * S + tt * 512, [[96 * S, 8], [S, 16], [1, 512]]), seltT.t[:],
                  reads=[seltT.r], writes=[r_QT[h][tt] for h in range(8)])
        if "v" in parts:
            sl, w3 = win.load(2)
            for sub in range(4):
                b = next_main()
                for kc in range(8):
                    mm(b, PS[b][:], hT.t[:, kc, sub * 128:(sub + 1) * 128], w3[:, kc, :], kc == 0, kc == 7, [hT.rc[kc], sl.r])
                o = nxt(obf, "obf")
                P.op("act", lambda e, b=b, o=o: e.activation(out=o.t[:], in_=PS[b][:], func=AF.Copy), [r_PS[b]], [o.r])
                P.dma("pool", VS[tt * 512 + sub * 128: tt * 512 + (sub + 1) * 128, 0:512], o.t[:],
                      reads=[o.r], writes=[r_VS[0][tt]])
        if "l" in parts:
            sl, w3 = win.load(3)
            bq = [next_main(), next_main()]
            for c2 in range(2):
                proj_fm(bq[c2], 128, lambda kc, c2=c2: w3[:, kc, c2 * 128:(c2 + 1) * 128], sl.r)
                s = nxt(sqb, "sqb")
                P.op("act", lambda e, s=s, b=bq[c2]: e.activation(out=s.t[:], in_=PS[b][:], func=AF.Square), [r_PS[bq[c2]]], [s.r])
                mm(4, PS[4][:], ones_bf.t[:], s.t[:], c2 == 0, c2 == 1, [ones_bf.r, s.r])
            a = nxt(t1, "t1")
            rs = nxt(rstd, "rstd")
            P.op("act", lambda e: e.activation(out=a.t[:], in_=PS[4][:], func=AF.Sqrt, bias=epsc.t[:, 0:1], scale=1.0 / 256),
                 [r_PS[4], epsc.r], [a.r])
            P.op("dve", lambda e: e.reciprocal(out=rs.t[:], in_=a.t[:]), [a.r], [rs.r])
            for c2 in range(2):
                P.op("dve", lambda e, c2=c2, b=bq[c2]: e.scalar_tensor_tensor(out=qln.t[:, c2, :], in0=PS[b][:],
                                                                              scalar=ct.t[:, 18 + c2:19 + c2], in1=rs.t[:],
                                                                              op0=ALU.mult, op1=ALU.mult),
                     [r_PS[bq[c2]], ct.r, rs.r], [qln.r])
            b = next_main()
            proj_fm(b, 128, lambda kc: w3[:, kc, 256:384], sl.r)
            headnorm(b, 128, ones_bf, 1.0 / 128, 0, ct, 20, kvn.t[:], kvn.r)
            proj_fm(7, 96, lambda kc: w3[:, kc, 320:416], sl.r)
            for kd in kdr:
                P.op("act", lambda e, kd=kd: e.activation(out=kd.t[64:96, :], in_=PS[7][64:96, :], func=AF.Copy), [r_PS[7]], [kd.r])
            for sq in sq96s:
                P.op("act", lambda e, sq=sq: e.activation(out=sq.t[64:96, :], in_=PS[7][64:96, :], func=AF.Square), [r_PS[7]], [sq.r])

        def rope_store(y, dst, hidx, rl):
            mm(7, PS[7][0:96, :], perm.t[:], y.t[:], True, True, [perm.r, y.r])
            P.op("pool", lambda e: e.tensor_tensor(out=rt1.t[:], in0=y.t[:], in1=ropc.t[:], op=ALU.mult),
                 [y.r, ropc.r], [rt1.r])
            P.op("dve", lambda e: e.tensor_tensor(out=rt2.t[:], in0=PS[7][0:96, :], in1=rops.t[:], op=ALU.mult),
                 [r_PS[7], rops.r], [rt2.r])
            o = nxt(obf, "obf")
            P.op("dve", lambda e: e.tensor_tensor(out=o.t[0:96, :], in0=rt1.t[:], in1=rt2.t[:], op=ALU.add),
                 [rt1.r, rt2.r], [o.r])
            P.dma("pool", dst[hidx, 0:96, tt * 512:(tt + 1) * 512], o.t[0:96, :], reads=[o.r], writes=[rl[hidx][tt]])

        if "Q" in parts:
            pend = None
            for h in range(8):
                b = next_main()
                for kc in range(2):
                    mm(b, PS[b][0:96, :], wqb_sb.t[:, kc, h * 96:(h + 1) * 96], qln.t[:, kc, :], kc == 0, kc == 1,
                       [wqb_sb.r, qln.r])
                y = nxt(yf, "yf")
                headnorm(b, 96, ones_bf, 1.0, 2, ct, 21, None, None, extra_f32=y)
                if pend is not None:
                    rope_store(pend[0], QT, 8 + pend[1], r_QT)
                pend = (y, h)
            rope_store(pend[0], QT, 8 + pend[1], r_QT)
        if "K" in parts:
            pend = None
            for h in range(8):
                b = next_main()
                mm(b, PS[b][0:64, :], wkvb_sb.t[:, h * 128:h * 128 + 64], kvn.t[:], True, True, [wkvb_sb.r, kvn.r])
                kd = nxt(kdr, "kdr")
                sq = nxt(sq96s, "sq96s")
                P.op("act", lambda e, b=b, kd=kd: e.activation(out=kd.t[0:64, :], in_=PS[b][0:64, :], func=AF.Copy),
                     [r_PS[b]], [kd.r])
                P.op("act", lambda e, b=b, sq=sq: e.activation(out=sq.t[0:64, :], in_=PS[b][0:64, :], func=AF.Square),
                     [r_PS[b]], [sq.r])
                mm(4, PS[4][0:96, :], ones_bf.t[0:96, 0:96], sq.t[:], True, True, [ones_bf.r, sq.r])
                a = nxt(t1, "t1")
                rs = nxt(rstd, "rstd")
                P.op("act", lambda e, a=a: e.activation(out=a.t[0:96, :], in_=PS[4][0:96, :], func=AF.Sqrt,
                                                        bias=epsc.t[0:96, 0:1], scale=1.0 / 96), [r_PS[4], epsc.r], [a.r])
                P.op("dve", lambda e, a=a, rs=rs: e.reciprocal(out=rs.t[0:96, :], in_=a.t[0:96, :]), [a.r], [rs.r])
                y = nxt(yf, "yf")
                P.op("dve", lambda e, y=y, rs=rs, kd=kd: e.scalar_tensor_tensor(out=y.t[:], in0=kd.t[:], scalar=ct.t[0:96, 22:23],
                                                                                in1=rs.t[0:96, :], op0=ALU.mult, op1=ALU.mult),
                     [kd.r, ct.r, rs.r], [y.r])
                if pend is not None:
                    rope_store(pend[0], KT, 8 + pend[1], r_KT)
                pend = (y, h)
            rope_store(pend[0], KT, 8 + pend[1], r_KT)
        if "V" in parts:
            wv = wkvb_sb.t[:].rearrange("p (h c) -> p h c", c=128)[:, :, 64:128]
            for sub in range(4):
                b = next_main()
                mm(b, PS[b][:].rearrange("p (h c) -> p h c", c=64), kvn.t[:, sub * 128:(sub + 1) * 128], wv, True, True,
                   [kvn.r, wkvb_sb.r])
                o = nxt(obf, "obf")
                P.op("act", lambda e, b=b, o=o: e.activation(out=o.t[:], in_=PS[b][:], func=AF.Copy), [r_PS[b]], [o.r])
                P.dma("pool", VS[tt * 512 + sub * 128: tt * 512 + (sub + 1) * 128, 512:1024], o.t[:],
                      reads=[o.r], writes=[r_VS[1][tt]])

    def outproj_ffn(l, tt, x, mixT, ln):
        for half in range(2):
            sl, w3 = WOUT[l].load(half)
            for ch in range(4):
                dc = half * 4 + ch
                b = next_main()
                proj_fm(b, 128, lambda kc, ch=ch: w3[:, kc, ch * 128:(ch + 1) * 128], sl.r, rhs_t=mixT)
                P.op("dve", lambda e, b=b, dc=dc: e.tensor_tensor(out=x.t[:, dc, :], in0=PS[b][:], in1=x.t[:, dc, :],
                                                                  op=ALU.add), [r_PS[b], x.r], [x.r])
            norm_squares(x, range(half * 4, half * 4 + 4))
        norm_finish(x, colt[l], 8)
        for gi, (f0, f1) in enumerate(FGRP):
            slg, wg3 = WGU[l].load(gi)
            slu, wu3 = WGU[l].load(len(FGRP) + gi)
            for fl in range(f1 - f0):
                fc = f0 + fl
                bg = 5 + (fc % 2)
                bu = next_main()
                for kc in range(8):
                    mm(bg, PS[bg][:], wg3[:, kc, fl * 128:(fl + 1) * 128], hT.t[:, kc, :], kc == 0, kc == 7, [slg.r, hT.rc[kc]])
                for kc in range(8):
                    mm(bu, PS[bu][:], wu3[:, kc, fl * 128:(fl + 1) * 128], hT.t[:, kc, :], kc == 0, kc == 7, [slu.r, hT.rc[kc]])
                s = nxt(sg, "sg")
                P.op("act", lambda e, s=s, bg=bg: e.activation(out=s.t[:], in_=PS[bg][:], func=AF.Silu), [r_PS[bg]], [s.r])
                P.op("dve", lambda e, s=s, bu=bu, fc=fc: e.tensor_tensor(out=actT.t[:, fc, :], in0=PS[bu][:], in1=s.t[:],
                                                                         op=ALU.mult), [r_PS[bu], s.r], [actT.r])
        dbanks = [0, 1, 2, 7]
        for dh in range(2):
            for g, (f0, f1) in enumerate(DGRP):
                sl, w3 = WD[l][g].load(dh)
                for fl in range(f1 - f0):
                    fc = f0 + fl
                    for d4 in range(4):
                        b = dbanks[d4]
                        mm(b, PS[b][:], w3[:, fl, d4 * 128:(d4 + 1) * 128], actT.t[:, fc, :], fc == 0, fc == NF - 1,
                           [sl.r, actT.r])
            for d4 in range(4):
                b = dbanks[d4]
                dc = dh * 4 + d4
                P.op("dve", lambda e, b=b, dc=dc: e.tensor_tensor(out=x.t[:, dc, :], in0=PS[b][:], in1=x.t[:, dc, :], op=ALU.add),
                     [r_PS[b], x.r], [x.r])
            if ln is not None:
                norm_squares(x, range(dh * 4, dh * 4 + 4))
        if ln is not None:
            norm_finish(x, colt[ln], 0)

    out_ops = []

    def r_phase(lp, ln):
        for tt in range(NT):
            x = xt[tt % 2]
            if lp is None:
                for sub in range(4):
                    tk = nxt(tok, "tok")
                    r0 = tt * 512 + sub * 128
                    P.dma("pool", tk.t[:], x_in[r0:r0 + 128, :], writes=[tk.r])
                    for half in range(2):
                        bb = 5 + half
                        for c4 in range(4):
                            c = half * 4 + c4
                            P.op("pe", lambda e, bb=bb, c4=c4, c=c, tk=tk: e.transpose(PS[bb][:, c4 * 128:(c4 + 1) * 128],
                                                                                       tk.t[:, c * 128:(c + 1) * 128], ident.t[:]),
                                 [tk.r, ident.r], [r_PS[bb]])
                        P.op("dve" if half == 0 else "act",
                             (lambda e, bb=bb, half=half, sub=sub: e.tensor_copy(
                                 out=x.t[:, half * 4:(half + 1) * 4, sub * 128:(sub + 1) * 128],
                                 in_=PS[bb][:].rearrange("p (c t) -> p c t", t=128))) if half == 0 else
                             (lambda e, bb=bb, half=half, sub=sub: e.activation(
                                 out=x.t[:, half * 4:(half + 1) * 4, sub * 128:(sub + 1) * 128],
                                 in_=PS[bb][:].rearrange("p (c t) -> p c t", t=128), func=AF.Copy)),
                             [r_PS[bb]], [x.r])
            else:
                def _loads(t2):
                    P.dma("pool", xt[t2 % 2].t[:], XT[:, :, t2 * 512:(t2 + 1) * 512], reads=[r_XT[t2]], writes=[xt[t2 % 2].r])
                    P.dma("pool", mixTs[t2 % 2].t[:], MIX[:, :, t2 * 512:(t2 + 1) * 512], reads=[r_MIX[t2]],
                          writes=[mixTs[t2 % 2].r])
                if tt == 0:
                    _loads(0)
                if tt + 1 < NT:
                    _loads(tt + 1)
                outproj_ffn(lp, tt, x, mixTs[tt % 2], ln)
            if ln is not None:
                if lp is None:
                    norm_fm(x, colt[ln], 0)
                if ln % 2 == 0:
                    inproj_even(ln, tt)
                else:
                    inproj_odd(ln, tt)
                P.dma("pool", XT[:, :, tt * 512:(tt + 1) * 512], x.t[:], reads=[x.r], writes=[r_XT[tt]])
            else:
                for sub in range(4):
                    tk = nxt(tok, "tok")
                    for half in range(2):
                        bb = 5 + half
                        for c4 in range(4):
                            c = half * 4 + c4
                            P.op("pe", lambda e, bb=bb, c4=c4, c=c, sub=sub: e.transpose(
                                PS[bb][:, c4 * 128:(c4 + 1) * 128], x.t[:, c, sub * 128:(sub + 1) * 128], ident.t[:]),
                                 [x.r, ident.r], [r_PS[bb]])
                        if half == 0:
                            P.op("dve", lambda e, bb=bb, tk=tk: e.tensor_copy(out=tk.t[:, 0:512], in_=PS[bb][:]),
                                 [r_PS[bb]], [tk.r])
                        else:
                            P.op("act", lambda e, bb=bb, tk=tk: e.activation(out=tk.t[:, 512:1024], in_=PS[bb][:], func=AF.Copy),
                                 [r_PS[bb]], [tk.r])
                    r0 = tt * 512 + sub * 128
                    out_ops.append(P.dma("pool", out_d[r0:r0 + 128, :], tk.t[:], reads=[tk.r]))

    def load_head(hg, K, buf, moba_h=None):
        q, k, v = qsb[buf], ksb[buf], vaug[buf]
        P.dma("sp", q.t[0:K, :], QT[hg, 0:K, :], reads=r_QT[hg], writes=[q.r])
        P.dma("sp", k.t[0:K, :], KT[hg, 0:K, :], reads=r_KT[hg] + [r_KTaug[hg]], writes=[k.r])
        g = hg // 8
        P.dma("sp", v.t[:, :, 0:64], dap(VS, hg * 64, [[DM, 128], [128 * DM, 32], [1, 64]]),
              reads=r_VS[g], writes=[v.r])
        if moba_h is not None:
            for d5 in range(5):
                dd = 128 - 128 * d5
                P.dma("sp", ebh.t[:, d5, :], dap(GSR, moba_h * GLEN + (128 - dd), [[1, 128], [1, 512]]),
                      reads=[r_GSR], writes=[ebh.r])
            e0 = ebh.t[:, :, :]
            rev = bass.AP(tensor=e0.tensor, offset=e0.offset + 511, ap=[list(e0.ap[0]), list(e0.ap[1]), [-1, 512]])
            P.op("pool", lambda e: e.tensor_copy(out=eb[buf].t[:], in_=rev), [ebh.r], [eb[buf].r])

    def att_softmax(hg, K, buf, kind, bias_t=None, moba_h=None):
        q, k, v = qsb[buf], ksb[buf], vaug[buf]
        SBK = [(0, 1), (2, 3)]
        OBK = [4, 5]

        def s_mm(qt, kb, b):
            diag = kind != "moba" and kb >= 4 * qt
            mm(b, PS[b][:], k.t[0:K, kb * 128:(kb + 1) * 128], q.t[0:K, qt * 512:(qt + 1) * 512], True, not diag, [k.r, q.r])
            if diag:
                mm(b, PS[b][:], ident_bf.t[:], mask_le.t[:, kb - 4 * qt, :], False, True, [ident_bf.r, mask_le.r])

        for qt in range(NT):
            nkb = 4 * qt + 4
            npair = nkb // 2
            bo = OBK[qt % 2]

            def s_pair(pi):
                for j in range(2):
                    s_mm(qt, 2 * pi + j, SBK[pi % 2][j])

            s_pair(0)
            for pi in range(npair):
                if pi + 1 < npair:
                    s_pair(pi + 1)
                b0, b1 = SBK[pi % 2]
                kbp = (2 * pi, 2 * pi + 1)
                near = [kind == "moba" and kb >= 4 * qt - 1 for kb in kbp]
                if not any(near):
                    p = nxt(pt2, "pt2")
                    src = PSBIG[:, b0 * 512:(b0 + 2) * 512]
                    if kind == "moba":
                        P.op("act", lambda e, p=p, src=src: e.activation(out=p.t[:], in_=src, func=AF.Exp,
                                                                         bias=bias_t.t[:, 24 + moba_h:25 + moba_h], scale=1.0),
                             [r_PS[b0], r_PS[b1], bias_t.r], [p.r])
                    else:
                        P.op("act", lambda e, p=p, src=src: e.activation(out=p.t[:], in_=src, func=AF.Exp),
                             [r_PS[b0], r_PS[b1]], [p.r])
                    for j in range(2):
                        kb = kbp[j]
                        mm(bo, PS[bo][:], v.t[:, kb, :], p.t[:, j * 512:(j + 1) * 512], kb == 0, kb == nkb - 1, [v.r, p.r])
                else:
                    for j in range(2):
                        kb = kbp[j]
                        b = (b0, b1)[j]
                        p = nxt(pt, "pt")
                        if near[j]:
                            f = nxt(ef, "ef")
                            d5 = kb - (4 * qt - 1)
                            P.op("act", lambda e, b=b, f=f: e.activation(out=f.t[:], in_=PS[b][:], func=AF.Exp), [r_PS[b]], [f.r])
                            P.op("dve", lambda e, f=f, p=p, d5=d5: e.tensor_tensor(out=p.t[:], in0=f.t[:], in1=eb[buf].t[:, d5, :],
                                                                                   op=ALU.mult), [f.r, eb[buf].r], [p.r])
                        else:
                            P.op("act", lambda e, b=b, p=p: e.activation(out=p.t[:], in_=PS[b][:], func=AF.Exp,
                                                                         bias=bias_t.t[:, 24 + moba_h:25 + moba_h], scale=1.0),
                                 [r_PS[b], bias_t.r], [p.r])
                        mm(bo, PS[bo][:], v.t[:, kb, :], p.t[:], kb == 0, kb == nkb - 1, [v.r, p.r])
            rc = rec[qt % 2]
            o = ot[qt % 2]
            P.op("dve", lambda e, rc=rc, bo=bo: e.reciprocal(out=rc.t[:], in_=PS[bo][64:128, :]), [r_PS[bo]], [rc.r])
            P.op("dve", lambda e, rc=rc, bo=bo, o=o: e.tensor_tensor(out=o.t[:], in0=PS[bo][0:64, :], in1=rc.t[:], op=ALU.mult),
                 [r_PS[bo], rc.r], [o.r])
            c, p0 = divmod(hg * 64, 128)
            P.dma("pool", MIX[p0:p0 + 64, c, qt * 512:(qt + 1) * 512], o.t[:], reads=[o.r], writes=[r_MIX[qt]])

    def att_sb(hg, buf):
        q, k, v = qsb[buf], ksb[buf], vaug[buf]
        ZB = [0, 1]
        TB = [4, 5]
        OB = [6, 7]
        for pair in range(NT // 2):
            qts = [2 * pair, 2 * pair + 1]
            nk = [4 * qt + 4 for qt in qts]
            kbs = [list(range(n - 1, -1, -1)) for n in nk]
            nmax = nk[1]

            def active(n):
                return [s_ for s_ in range(2) if n < nk[s_]]

            def rng(n):
                a_ = active(n)
                return (a_[0] * 512, (a_[-1] + 1) * 512)

            def emit_z(n):
                for s_ in active(n):
                    kb = kbs[s_][n]
                    mm(ZB[s_], PS[ZB[s_]][:], k.t[0:64, kb * 128:(kb + 1) * 128],
                       q.t[0:64, qts[s_] * 512:(qts[s_] + 1) * 512], True, True, [k.r, q.r])

            def emit_e(n):
                f = ef2[n % 2]
                lo, hi = rng(n)
                P.op("act", lambda e, f=f, lo=lo, hi=hi: e.activation(out=f.t[:, lo:hi], in_=PSBIG[:, lo:hi], func=AF.Exp),
                     [r_PS[ZB[s_]] for s_ in active(n)], [f.r])
                for s_ in active(n):
                    kb = kbs[s_][n]
                    if kb >= 4 * qts[s_]:
                        r = kb - 4 * qts[s_]
                        P.op("pool", lambda e, f=f, r=r, s_=s_: e.tensor_tensor(out=f.t[:, s_ * 512:(s_ + 1) * 512],
                                                                               in0=f.t[:, s_ * 512:(s_ + 1) * 512],
                                                                               in1=mask_lt.t[:, r, :], op=ALU.mult),
                             [f.r, mask_lt.r], [f.r])
                return f

            def emit_sp(n, f):
                sp_ = spb2[n % 2]
                lo, hi = rng(n)
                P.op("act", lambda e, f=f, sp_=sp_, lo=lo, hi=hi: e.activation(out=sp_.t[:, lo:hi], in_=f.t[:, lo:hi], func=AF.Ln,
                                                                               bias=epsc.t[:, 3:4], scale=1.0),
                     [f.r, epsc.r], [sp_.r])
                return sp_

            def emit_t1(n, sp_):
                for s_ in active(n):
                    mm(TB[s_], PS[TB[s_]][:], triinc.t[:], sp_.t[:, s_ * 512:(s_ + 1) * 512], n == 0, False, [triinc.r, sp_.r])

            def emit_g(n):
                g = gf2[n % 2]
                lo, hi = rng(n)
                P.op("act", lambda e, g=g, lo=lo, hi=hi: e.activation(out=g.t[:, lo:hi], in_=PSBIG[:, 4 * 512 + lo:4 * 512 + hi],
                                                                      func=AF.Exp, scale=-1.0),
                     [r_PS[TB[s_]] for s_ in active(n)], [g.r])
                return g

            def emit_t2(n, sp_):
                for s_ in active(n):
                    mm(TB[s_], PS[TB[s_]][:], tristr.t[:], sp_.t[:, s_ * 512:(s_ + 1) * 512], False, n == nk[s_] - 1,
                       [tristr.r, sp_.r])

            def emit_p(n, f, g):
                p = nxt(pt2, "pt2")
                lo, hi = rng(n)
                P.op("dve", lambda e, p=p, f=f, g=g, lo=lo, hi=hi: e.tensor_tensor(out=p.t[:, lo:hi], in0=f.t[:, lo:hi],
                                                                                  in1=g.t[:, lo:hi], op=ALU.mult),
                     [f.r, g.r], [p.r])
                for s_ in active(n):
                    kb = kbs[s_][n]
                    mm(OB[s_], PS[OB[s_]][0:64, :], v.t[:, kb, 0:64], p.t[:, s_ * 512:(s_ + 1) * 512], n == 0, n == nk[s_] - 1,
                       [v.r, p.r])

            emit_z(0)
            f = emit_e(0)
            sp_ = emit_sp(0, f)
            if 1 < nmax:
                emit_z(1)
            emit_t1(0, sp_)
            for n in range(nmax):
                f_next = sp_next = None
                if n + 1 < nmax:
                    f_next = emit_e(n + 1)
                if n + 2 < nmax:
                    emit_z(n + 2)
                g = emit_g(n)
                emit_t2(n, sp_)
                if n + 1 < nmax:
                    sp_next = emit_sp(n + 1, f_next)
                    emit_t1(n + 1, sp_next)
                emit_p(n, f, g)
                f, sp_ = f_next, sp_next
            for s_ in range(2):
                o = ot[s_]
                P.op("dve", lambda e, o=o, s_=s_: e.tensor_copy(out=o.t[:], in_=PS[OB[s_]][0:64, :]), [r_PS[OB[s_]]], [o.r])
                c, p0 = divmod(hg * 64, 128)
                P.dma("pool", MIX[p0:p0 + 64, c, qts[s_] * 512:(qts[s_] + 1) * 512], o.t[:], reads=[o.r],
                      writes=[r_MIX[qts[s_]]])

    def att_phase(l, after_first_loads=None):
        even = l % 2 == 0
        phase_barrier()
        for i in range(2):
            P.op("pool", lambda e, i=i: e.memset(vaug[i].t[:, :, 64:128], 1.0), [], [vaug[i].r])
        if even:
            plan = [(h, 70, "fox") for h in range(8)] + [(8 + h, 64, "sb") for h in range(8)]
        else:
            plan = [(h, 80, "moba") for h in range(8)] + [(8 + h, 96, "mla") for h in range(8)]
        plan = [p_ for p_ in plan if p_[2] not in skip_kinds]
        if l in skip_att:
            plan = []
        for idx, (hg, K, kind) in enumerate(plan):
            buf = idx % 2
            if idx == 0:
                load_head(hg, K, buf, moba_h=hg if kind == "moba" else None)
            if idx + 1 < len(plan):
                nh, nK, nkind = plan[idx + 1]
                load_head(nh, nK, 1 - buf, moba_h=nh if nkind == "moba" else None)
            if idx == 0 and after_first_loads is not None:
                after_first_loads()
            if kind == "sb":
                att_sb(hg, buf)
            else:
                att_softmax(hg, K, buf, kind, bias_t=colt[l], moba_h=hg if kind == "moba" else None)

    def build_gsr():
        rbT = T("rbT", [32, 8], F32)
        P.dma("pool", rbT.t[:], rbT_d, writes=[rbT.r])
        for j, (a, b_) in enumerate(((0, 512), (512, 1024), (1024, GLEN))):
            n = b_ - a
            oh, sm, gr = sg[0], sg[1], obf[0]
            P.dma("pool", oh.t[0:32, 0:n], C["ohr"][:, a:b_], writes=[oh.r])
            P.dma("pool", sm.t[0:8, 0:n], C["stepmask"][:, a:b_], writes=[sm.r])
            mm(5, PS[5][0:8, 0:n], rbT.t[:], oh.t[0:32, 0:n], True, True, [rbT.r, oh.r])
            P.op("act", lambda e, n=n: e.activation(out=t1[0].t[0:8, 0:n], in_=PS[5][0:8, 0:n], func=AF.Exp),
                 [r_PS[5]], [t1[0].r])
            P.op("dve", lambda e, n=n, sm=sm, gr=gr: e.tensor_tensor(out=gr.t[0:8, 0:n], in0=t1[0].t[0:8, 0:n],
                                                                     in1=sm.t[0:8, 0:n], op=ALU.mult),
                 [t1[0].r, sm.r], [gr.r])
            P.dma("pool", GSR[:, a:b_], gr.t[0:8, 0:n], reads=[gr.r], writes=[r_GSR])

    def moba_kaug():
        kh = ksb[0]
        P.dma("pool", kh.t[0:16, :], C["kblockhot"], writes=[kh.r])
        for h in range(8):
            P.dma("pool", KT[h, 64:80, :], kh.t[0:16, :], reads=[kh.r], writes=[r_KTaug[h]])

    conv_layer(0)
    r_phase(None, 0)
    for l in range(n_layers):
        def _conv(l=l):
            if l == 0:
                conv_layer_post(0)
            if l + 1 < n_layers:
                conv_layer(l + 1)
                conv_layer_post(l + 1)
        att_phase(l, _conv)
        if l + 1 < n_layers and (l + 1) % 2 == 1:
            if "a" in parts:
                moba_kaug()
        phase_barrier()
        if l == 0 and n_layers > 1 and "g" in parts:
            build_gsr()
        r_phase(l, l + 1 if l + 1 < n_layers else None)
    cnt = P.finalize_and_emit(out_ops)
    st.close()
    return nc, cnt


_CACHE = {}


def make_in_maps(inputs, n_layers=4):
    consts = host_consts()
    shared = {}
    shared["ffn_w_gate_up"] = np.ascontiguousarray(inputs["ffn_w_gate_up"], np.float32)
    shared["ffn_w_down"] = np.ascontiguousarray(inputs["ffn_w_down"], np.float32)
    shared["ev_w_in"] = np.ascontiguousarray(inputs["ev_w_in"], np.float32)
    shared["ev_w_out"] = np.ascontiguousarray(inputs["ev_w_out"], np.float32)
    shared["od_w_in"] = np.ascontiguousarray(inputs["od_w_in"], np.float32)
    shared["od_w_out"] = np.ascontiguousarray(inputs["od_w_out"], np.float32)
    shared["od_mla_w_q_b"] = np.ascontiguousarray(inputs["od_mla_w_q_b"], np.float32)
    shared["od_mla_w_kv_b"] = np.ascontiguousarray(inputs["od_mla_w_kv_b"], np.float32)
    shared["cols"] = np.stack([host_cols(inputs, l) for l in range(4)]).astype(np.float32)
    shared["rbT"] = np.ascontiguousarray(np.asarray(inputs["rel_bias"], np.float32).T)
    for k, v in consts.items():
        shared["c_" + k] = np.ascontiguousarray(v, np.float32)
    x = np.asarray(inputs["x"], np.float32)
    maps = []
    for b in range(8):
        m = dict(shared)
        m["x"] = np.ascontiguousarray(x[b])
        maps.append(m)
    return maps


def kernel(**inputs):
    inputs = {k: np.asarray(v) for k, v in inputs.items()}
    if "nc" not in _CACHE:
        _CACHE["nc"] = build(4)[0]
    nc = _CACHE["nc"]
    maps = make_in_maps(inputs)
    res = run_bass_kernel_spmd(nc, maps, core_ids=list(range(8)))
    out = np.stack([np.asarray(r["out"], np.float32) for r in res.results], axis=0)
    return out
```

```python
import contextlib
import math
import types
import numpy as np
import concourse.bass as bass
import concourse.mybir as mybir
from concourse.bass_utils import run_bass_kernel_spmd

F32 = mybir.dt.float32
BF16 = mybir.dt.bfloat16
ALU = mybir.AluOpType
AF = mybir.ActivationFunctionType
AX = mybir.AxisListType

S = 4096
DM = 1024
DFF = 2816
NF = DFF // 128
NT = 8
EVEN_IN = 3080
ODD_IN = 1952
EPS = 1e-6
BIGNEG = 32768.0
GLEN = 1151

ENGS = ("pe", "act", "dve", "pool", "sp")
DMA_SEMS = {"sp": 12, "pool": 16, "act": 2}
SEM_EPOCH = 20000


class Res:
    __slots__ = ("name", "last_w", "readers")

    def __init__(self, name):
        self.name = name
        self.last_w = None
        self.readers = []


class Op:
    __slots__ = ("eng", "fn", "deps", "is_dma", "sig", "has_dep", "idx")

    def __init__(self, eng, fn, is_dma):
        self.eng = eng
        self.fn = fn
        self.deps = set()
        self.is_dma = is_dma
        self.sig = None
        self.has_dep = False
        self.idx = -1


def _freeze(fn):
    if fn.__closure__ is None:
        return fn
    cells = []
    for c in fn.__closure__:
        try:
            cells.append(types.CellType(c.cell_contents))
        except ValueError:
            cells.append(c)
    return types.FunctionType(fn.__code__, fn.__globals__, fn.__name__, fn.__defaults__, tuple(cells))


class Prog:
    def __init__(self, nc):
        self.nc = nc
        self.ops = []

    def res(self, name="r"):
        return Res(name)

    def op(self, eng, fn, reads=(), writes=(), is_dma=False):
        o = Op(eng, _freeze(fn), is_dma)
        o.idx = len(self.ops)
        for r in reads:
            if r.last_w is not None:
                o.deps.add(r.last_w)
        for r in writes:
            if r.last_w is not None:
                o.deps.add(r.last_w)
            for rd in r.readers:
                o.deps.add(rd)
        o.deps.discard(o.idx)
        for r in reads:
            r.readers.append(o.idx)
        for r in writes:
            r.last_w = o.idx
            r.readers = []
        self.ops.append(o)
        return o.idx

    def dma(self, eng, out, in_, reads=(), writes=()):
        return self.op(eng, lambda e: e.dma_start(out=out, in_=in_), reads, writes, is_dma=True)

    def finalize_and_emit(self, out_ops=()):
        nc = self.nc
        ops = self.ops
        dma_last = {}
        dma_val = {}
        nd = {q: 0 for q in DMA_SEMS}
        for o in ops:
            if o.is_dma:
                j = (o.eng, nd[o.eng] % DMA_SEMS[o.eng])
                nd[o.eng] += 1
                if dma_last.get(j) is not None:
                    o.deps.add(dma_last[j])
                dma_last[j] = o.idx
                dma_val[j] = dma_val.get(j, 0) + 16
                o.sig = ("dma", j, dma_val[j])
        for o in ops:
            for d in o.deps:
                ops[d].has_dep = True
        for i in out_ops:
            ops[i].has_dep = True
        cnt = {e: 0 for e in ENGS}
        for o in ops:
            if not o.is_dma and o.has_dep:
                cnt[o.eng] += 1
                ep, v = divmod(cnt[o.eng] - 1, SEM_EPOCH)
                o.sig = (o.eng, ep, v + 1)
        n_ep = {e: max(1, (cnt[e] + SEM_EPOCH - 1) // SEM_EPOCH) for e in ENGS}
        with contextlib.ExitStack() as st:
            sems = {}
            for e in ENGS:
                for ep in range(n_ep[e]):
                    sems[(e, ep)] = st.enter_context(nc.semaphore(f"s_{e}_{ep}"))
            for q, n in DMA_SEMS.items():
                for j in range(n):
                    sems[("dma", (q, j))] = st.enter_context(nc.semaphore(f"s_dma_{q}_{j}"))
            block = st.enter_context(nc.Block())
            by_eng = {e: [o for o in ops if o.eng == e] for e in ENGS}

            def make(ename):
                def body(eng):
                    waited = {}
                    for o in by_eng[ename]:
                        need = {}
                        for d in o.deps:
                            od = ops[d]
                            if od.eng == "pe" and ename == "pe" and not od.is_dma and not o.is_dma:
                                continue
                            k = od.sig[:2]
                            v = od.sig[2]
                            if waited.get(k, 0) >= v:
                                continue
                            if need.get(k, 0) < v:
                                need[k] = v
                        for k, v in need.items():
                            eng.wait_ge(sems[k], v)
                            waited[k] = v
                        ins = o.fn(eng)
                        if o.is_dma:
                            ins.then_inc(sems[o.sig[:2]], 16)
                        elif o.has_dep:
                            ins.then_inc(sems[o.sig[:2]], 1)
                    if ename == "sp":
                        for i in out_ops:
                            od = ops[i]
                            eng.wait_ge(sems[od.sig[:2]], od.sig[2])
                return body

            block.tensor(make("pe"))
            block.scalar(make("act"))
            block.vector(make("dve"))
            block.gpsimd(make("pool"))
            block.sync(make("sp"))
        return cnt


def _t5_bucket_np(rel):
    n = np.maximum(rel, 0).astype(np.int32)
    nf = np.maximum(n, 1).astype(np.float32)
    large = 16 + (np.log(nf / np.float32(16)) / np.float32(math.log(128 / 16)) * np.float32(16)).astype(np.int32)
    large = np.minimum(large, 31)
    return np.where(n < 16, n, large)


def host_consts():
    c = {}
    c["ident"] = np.eye(128, dtype=np.float32)
    p = np.arange(128)[:, None]
    j = np.arange(512)[None, :]
    mle = np.zeros((128, 4, 512), np.float32)
    mlt = np.zeros((128, 4, 512), np.float32)
    cum = np.zeros((128, 4, 512), np.float32)
    for r in range(4):
        mle[:, r, :] = np.where(j >= p + 128 * r, 0.0, -30000.0)
        mlt[:, r, :] = (j > p + 128 * r)
        cum[:, r, :] = (p + 128 * r <= j)
    c["mask_le"] = mle
    c["mask_lt"] = mlt
    c["cum"] = cum
    jj = np.arange(128)[:, None]
    kk = np.arange(128)[None, :]
    c["triinc"] = (jj >= kk).astype(np.float32)
    c["tristr"] = (jj < kk).astype(np.float32)
    half = 16
    inv_freq = (np.float32(10000.0) ** (-np.arange(half, dtype=np.float32) / np.float32(half))).astype(np.float32)
    ang = np.arange(S, dtype=np.float32)[:, None] * inv_freq[None, :]
    cos = np.cos(ang).astype(np.float32).T
    sin = np.sin(ang).astype(np.float32).T
    rc = np.ones((96, S), np.float32)
    rs = np.zeros((96, S), np.float32)
    rc[64:80] = cos
    rc[80:96] = cos
    rs[64:80] = sin
    rs[80:96] = sin
    c["rope_c"] = rc
    c["rope_s"] = rs
    perm = np.zeros((96, 96), np.float32)
    for i in range(16):
        perm[80 + i, 64 + i] = -1.0
        perm[64 + i, 80 + i] = 1.0
    c["perm"] = perm
    v = np.arange(GLEN)
    rel = 639 - v
    bk = _t5_bucket_np(rel)
    oh = np.zeros((32, GLEN), np.float32)
    for b in range(32):
        oh[b] = ((bk == b) & (rel >= 0))
    c["ohr"] = oh
    c["stepmask"] = np.broadcast_to((rel >= 0).astype(np.float32)[None, :], (8, GLEN)).copy()
    s = np.arange(S)[None, :]
    n = np.arange(16)[:, None]
    c["kblockhot"] = (BIGNEG * ((s // 256) == n)).astype(np.float32)
    return c


CONST_SHAPES = {
    "ident": [128, 128], "mask_le": [128, 4, 512], "mask_lt": [128, 4, 512], "cum": [128, 4, 512],
    "triinc": [128, 128], "tristr": [128, 128], "rope_c": [96, S], "rope_s": [96, S], "perm": [96, 96],
    "ohr": [32, GLEN], "stepmask": [8, GLEN], "kblockhot": [16, S],
}

NCOLS = 32


def host_cols(inp, layer):
    t = np.zeros((128, NCOLS), np.float32)
    i = layer // 2
    fm = lambda v: np.asarray(v, np.float32).reshape(-1, 128).T
    t[:, 8:16] = fm(inp["ffn_norm"][layer])
    if layer % 2 == 0:
        t[:, 0:8] = fm(inp["ev_norm"][i])
        t[:, 16] = np.tile(np.asarray(inp["ev_fox_q_norm"][i], np.float32), 2)
        t[:, 17] = np.tile(np.asarray(inp["ev_fox_k_norm"][i], np.float32), 2)
        t[:, 24:32] = np.broadcast_to(np.asarray(inp["ev_forget_bias"][i], np.float32)[None, :], (128, 8))
    else:
        t[:, 0:8] = fm(inp["od_norm"][i])
        t[:, 16] = np.tile(np.asarray(inp["od_moba_q_norm"][i], np.float32), 2)
        t[:, 17] = np.tile(np.asarray(inp["od_moba_k_norm"][i], np.float32), 2)
        t[:, 18:20] = fm(inp["od_mla_q_a_norm"][i])
        t[:, 20] = np.asarray(inp["od_mla_kv_a_norm"][i], np.float32)
        t[0:96, 21] = np.asarray(inp["od_mla_q_norm"][i], np.float32)
        t[0:96, 22] = np.asarray(inp["od_mla_k_norm"][i], np.float32)
        t[:, 24:32] = np.broadcast_to(np.asarray(inp["rel_bias"], np.float32)[:, 31][None, :], (128, 8))
    return t


def build(n_layers=4, dbg=False, skip_att=(), skip_kinds=(), parts="kqsvlQKVga"):
    nc = bass.Bass("TRN2", target_bir_lowering=False)
    P = Prog(nc)
    st = contextlib.ExitStack()

    def din(name, shape):
        return nc.dram_tensor(name, list(shape), F32, kind="ExternalInput").ap()

    def dscr(name, shape, dt):
        kind = "ExternalOutput" if (dbg and name in ("QT", "KT", "VS", "MIX", "XT", "GSR")) else "Internal"
        return nc.dram_tensor(name, list(shape), dt, kind=kind).ap()

    def dap(base, offset, ap):
        return bass.AP(tensor=base.tensor, offset=base.offset + offset, ap=[list(a) for a in ap])

    x_in = din("x", [S, DM])
    out_d = nc.dram_tensor("out", [S, DM], F32, kind="ExternalOutput").ap()
    W = {}
    W["ffn_gu"] = din("ffn_w_gate_up", [4, DM, 2 * DFF])
    W["ffn_d"] = din("ffn_w_down", [4, DFF, DM])
    W["ev_in"] = din("ev_w_in", [2, DM, EVEN_IN])
    W["ev_out"] = din("ev_w_out", [2, DM, DM])
    W["od_in"] = din("od_w_in", [2, DM, ODD_IN])
    W["od_out"] = din("od_w_out", [2, DM, DM])
    W["qb"] = din("od_mla_w_q_b", [2, 256, 768])
    W["kvb"] = din("od_mla_w_kv_b", [2, 128, 1024])
    cols_d = din("cols", [4, 128, NCOLS])
    rbT_d = din("rbT", [32, 8])
    C = {k: din("c_" + k, shp) for k, shp in CONST_SHAPES.items()}

    XT = dscr("XT", [128, 8, S], F32)
    QT = dscr("QT", [16, 96, S], BF16)
    KT = dscr("KT", [16, 96, S], BF16)
    VS = dscr("VS", [S, DM], BF16)
    MIX = dscr("MIX", [128, 8, S], BF16)
    GSR = dscr("GSR", [8, GLEN], BF16)
    conv_thr = [P.res(f"convthr{i}") for i in range(4)]
    conv_ctr = [0]

    def conv_dma(dst, src, writes):
        j = conv_ctr[0] % 4
        conv_ctr[0] += 1
        P.dma("pool", dst, src, writes=list(writes) + [conv_thr[j]])

    class CW:
        def __init__(self, name, src2d, nk, chunks):
            self.src, self.nk, self.chunks = src2d, nk, chunks
            self.off = []
            tot = 0
            for (c0, n) in chunks:
                self.off.append(tot)
                tot += 128 * nk * n
            self.d = dscr(name, [tot], BF16)
            self.rs = [[P.res(name) for _ in range(nk)] for _ in chunks]

        def convert(self):
            nk = self.nk
            ci = 0
            while ci < len(self.chunks):
                c0, n = self.chunks[ci]
                cj = ci
                while (cj + 1 < len(self.chunks) and self.chunks[cj + 1][1] == n
                       and self.chunks[cj + 1][0] == self.chunks[cj][0] + n):
                    cj += 1
                nch = cj - ci + 1
                for kc in range(nk):
                    dst = dap(self.d, self.off[ci] + kc * n, [[nk * n, 128], [128 * nk * n, nch], [1, n]])
                    src = self.src[kc * 128:(kc + 1) * 128, c0:c0 + nch * n].rearrange("p (c n) -> p c n", n=n)
                    conv_dma(dst, src, [self.rs[c][kc] for c in range(ci, cj + 1)])
                ci = cj + 1

        def load(self, ci):
            c0, n = self.chunks[ci]
            nk = self.nk
            src = dap(self.d, self.off[ci], [[nk * n, 128], [1, nk * n]])
            sl = wslot_load([(lambda t: t[:, 0:nk * n], src)], self.rs[ci])
            return sl, v3(sl.t, nk, n)

    EV_CH = [(0, 512), (512, 512), (1024, 512), (1536, 8), (1544, 512), (2056, 512), (2568, 512)]
    OD_CH = [(0, 512), (512, 512), (1024, 512), (1536, 416)]
    FGRP = [(0, 4), (4, 8), (8, 12), (12, 16), (16, 20), (20, 22)]
    GU_CH = [(f0 * 128, (f1 - f0) * 128) for f0, f1 in FGRP] + [(DFF + f0 * 128, (f1 - f0) * 128) for f0, f1 in FGRP]
    DGRP = [(0, 8), (8, 16), (16, 22)]
    WIN, WOUT, WGU, WD = [], [], [], []
    for l in range(4):
        i = l // 2
        WIN.append(CW(f"WIN{l}", (W["ev_in"] if l % 2 == 0 else W["od_in"])[i], 8, EV_CH if l % 2 == 0 else OD_CH))
        WOUT.append(CW(f"WOUT{l}", (W["ev_out"] if l % 2 == 0 else W["od_out"])[i], 8, [(0, 512), (512, 512)]))
        WGU.append(CW(f"WGU{l}", W["ffn_gu"][l], 8, GU_CH))
        WD.append([CW(f"WD{l}_{g}", W["ffn_d"][l][f0 * 128:f1 * 128, :], f1 - f0, [(0, 512), (512, 512)])
                   for g, (f0, f1) in enumerate(DGRP)])
    WQB = [dscr(f"WQB{i}", [128, 2, 768], BF16) for i in range(2)]
    WKVB = [dscr(f"WKVB{i}", [128, 1024], BF16) for i in range(2)]
    r_XT = [P.res() for _ in range(NT)]
    r_QT = [[P.res() for _ in range(NT)] for _ in range(16)]
    r_KT = [[P.res() for _ in range(NT)] for _ in range(16)]
    r_KTaug = [P.res() for _ in range(16)]
    r_VS = [[P.res() for _ in range(NT)] for _ in range(2)]
    r_MIX = [P.res() for _ in range(NT)]
    r_GSR = P.res()
    r_W = {}

    def sb(name, shape, dt):
        t = st.enter_context(nc.sbuf_tensor("sb_" + name, list(shape), dt))
        return t

    ARENA_BYTES = 146 * 1024
    arena = sb("arena", [128, ARENA_BYTES // 2], BF16)
    aoff = {"R": 0, "A": 0}
    phase_res = {"R": [], "A": []}

    class T:
        def __init__(self, name, shape, dt, phase=None):
            self.r = P.res(name)
            if phase is None:
                self.t = sb(name, shape, dt)
                return
            esz = 4 if dt == F32 else 2
            n = 1
            for d in shape[1:]:
                n *= d
            nbytes = (n * esz + 63) // 64 * 64
            o = aoff[phase]
            aoff[phase] = o + nbytes
            assert aoff[phase] <= ARENA_BYTES, (name, phase, aoff[phase])
            v = arena[0:shape[0], o // 2:o // 2 + n * esz // 2]
            if dt == F32:
                v = v.bitcast(F32)
            if len(shape) == 3:
                v = v.rearrange("p (a b) -> p a b", b=shape[2])
            self.t = v
            phase_res[phase].append(self.r)

    barscr = sb("barscr", [128, 8], F32)

    def phase_barrier():
        P.op("pool", lambda e: e.memset(barscr[:], 0.0), [], phase_res["R"] + phase_res["A"])

    PSBIG = st.enter_context(nc.psum_tensor("psbig", [128, 4096], F32))
    PS = [PSBIG[:, i * 512:(i + 1) * 512] for i in range(8)]
    r_PS = [P.res(f"ps{i}") for i in range(8)]

    ident = T("ident", [128, 128], F32)
    ones_bf = T("ones_bf", [128, 128], BF16)
    ident_bf = T("ident_bf", [128, 128], BF16)
    blk64 = T("blk64", [128, 128], BF16)
    mask_le = T("mask_le", [128, 4, 512], BF16)
    mask_lt = T("mask_lt", [128, 4, 512], BF16)
    cumc = T("cumc", [128, 4, 512], F32)
    triinc = T("triinc", [128, 128], BF16)
    tristr = T("tristr", [128, 128], BF16)
    perm = T("perm", [96, 96], F32)
    colt = [T(f"cols{l}", [128, NCOLS], F32) for l in range(4)]
    epsc = T("epsc", [128, 4], F32)
    zero128 = T("zero128", [128, 128], F32)
    onesrow = T("onesrow", [8, 512], BF16)
    negrow = T("negrow", [8, 512], BF16)

    def cdma(tile, src):
        P.dma("pool", tile.t[:], src, writes=[tile.r])

    cdma(ident, C["ident"])
    cdma(mask_le, C["mask_le"])
    cdma(mask_lt, C["mask_lt"])
    cdma(cumc, C["cum"])
    cdma(triinc, C["triinc"])
    cdma(tristr, C["tristr"])
    cdma(perm, C["perm"])
    for l in range(n_layers):
        P.dma("pool", colt[l].t[:], cols_d[l], writes=[colt[l].r])
    P.op("dve", lambda e: e.memset(ones_bf.t[:], 1.0), [], [ones_bf.r])
    P.op("dve", lambda e: e.tensor_copy(out=ident_bf.t[:], in_=ident.t[:]), [ident.r], [ident_bf.r])
    P.op("dve", lambda e: e.memset(blk64.t[:], 0.0), [], [blk64.r])
    P.op("dve", lambda e: e.memset(blk64.t[0:64, 0:64], 1.0), [], [blk64.r])
    P.op("dve", lambda e: e.memset(blk64.t[64:128, 64:128], 1.0), [], [blk64.r])
    P.op("dve", lambda e: e.memset(epsc.t[:, 0:1], EPS), [], [epsc.r])
    P.op("dve", lambda e: e.memset(epsc.t[:, 1:2], 64 * EPS), [], [epsc.r])
    P.op("dve", lambda e: e.memset(epsc.t[:, 2:3], 96 * EPS), [], [epsc.r])
    P.op("dve", lambda e: e.memset(epsc.t[:, 3:4], 1.0), [], [epsc.r])
    P.op("dve", lambda e: e.memset(zero128.t[:], 0.0), [], [zero128.r])
    P.op("dve", lambda e: e.memset(onesrow.t[:], 1.0), [], [onesrow.r])
    P.op("dve", lambda e: e.memset(negrow.t[:], -1.0), [], [negrow.r])

    def conv2d(dst, src, nk, key):
        rl = [P.res(key) for _ in range(nk)]
        r_W[key] = rl
        for kc in range(nk):
            conv_dma(dst[:, kc, :], src[kc * 128:(kc + 1) * 128, :], [rl[kc]])

    def conv_layer(l):
        i = l // 2
        WIN[l].convert()
        if l % 2 == 1:
            conv2d(WQB[i], W["qb"][i], 2, f"wqb{i}")
            r = P.res()
            r_W[f"wkvb{i}"] = r
            P.dma("pool", WKVB[i], W["kvb"][i], writes=[r])

    def conv_layer_post(l):
        WOUT[l].convert()
        WGU[l].convert()
        for cw in WD[l]:
            cw.convert()

    NSLOT = 5
    wslots = [T(f"wslot{i}", [128, 4096], BF16) for i in range(NSLOT)]
    wctr = [0]

    def wload(parts):
        sl = wslots[wctr[0] % NSLOT]
        wctr[0] += 1
        return sl

    xt = [T(f"xt{i}", [128, 8, 512], F32, "R") for i in range(2)]
    hT = T("hT", [128, 8, 512], BF16, "R")
    hT.rc = [P.res(f"hT{c}") for c in range(8)]
    phase_res["R"].extend(hT.rc)
    mixTs = [T(f"mixT{i}", [128, 8, 512], BF16, "R") for i in range(2)]
    actT = T("actT", [128, NF, 512], BF16, "R")
    sqb = [T(f"sqb{i}", [128, 512], BF16, "R") for i in range(2)]
    t1 = [T(f"t1_{i}", [128, 512], F32, "R") for i in range(2)]
    rstd = [T(f"rstd{i}", [128, 512], F32, "R") for i in range(2)]
    obf = [T(f"obf{i}", [128, 512], BF16, "R") for i in range(3)]
    sg = [T(f"sg{i}", [128, 512], F32, "R") for i in range(2)]
    tok = [T(f"tok{i}", [128, 1024], F32, "R") for i in range(2)]
    fx = T("fx", [128, 8], F32, "R")
    nlf = T("nlf", [128, 4, 8], F32, "R")
    carry = T("carry", [8, 1], F32, "R")
    cT = T("cT", [8, 512], F32, "R")
    csp = [T(f"csp{i}", [8, 512], BF16, "R") for i in range(3)]
    crem = [T(f"crem{i}", [8, 512], F32, "R") for i in range(2)]
    qf = [T(f"qf{i}", [128, 512], F32, "R") for i in range(2)]
    KM = T("KM", [128, 4, 16], F32, "R")
    gsb = T("gsb", [128, 8, 16], F32, "R")
    mx8 = T("mx8", [128, 8, 8], F32, "R")
    selt = T("selt", [128, 8, 16], F32, "R")
    selm1 = T("selm1", [128, 8, 16], F32, "R")
    seltT = T("seltT", [128, 512], BF16, "R")
    qln = T("qln", [128, 2, 512], BF16, "R")
    kvn = T("kvn", [128, 512], BF16, "R")
    wqb_sb = T("wqb_sb", [128, 2, 768], BF16, "R")
    wkvb_sb = T("wkvb_sb", [128, 1024], BF16, "R")
    ropc = T("ropc", [96, 512], F32, "R")
    rops = T("rops", [96, 512], F32, "R")
    yf = [T(f"yf{i}", [96, 512], F32, "R") for i in range(2)]
    kdr = [T(f"kdraw{i}", [96, 512], F32, "R") for i in range(2)]
    sq96s = [T(f"sq96_{i}", [96, 512], BF16, "R") for i in range(2)]
    rt1 = T("rt1", [96, 512], F32, "R")
    rt2 = T("rt2", [96, 512], F32, "R")

    qsb = [T(f"qsb{i}", [96, S], BF16, "A") for i in range(2)]
    ksb = [T(f"ksb{i}", [96, S], BF16, "A") for i in range(2)]
    vaug = [T(f"vaug{i}", [128, 32, 128], BF16, "A") for i in range(2)]
    pt = [T(f"pt{i}", [128, 512], BF16, "A") for i in range(2)]
    ef = [T(f"ef{i}", [128, 512], F32, "A") for i in range(2)]
    pt2 = [T(f"pt2_{i}", [128, 1024], BF16, "A") for i in range(3)]
    ef2 = [T(f"ef2_{i}", [128, 1024], F32, "A") for i in range(2)]
    gf2 = [T(f"gf2_{i}", [128, 1024], F32, "A") for i in range(2)]
    spb2 = [T(f"spb2_{i}", [128, 1024], BF16, "A") for i in range(2)]
    rec = [T(f"rec{i}", [64, 512], F32, "A") for i in range(2)]
    ot = [T(f"ot{i}", [64, 512], BF16, "A") for i in range(2)]
    ebh = T("ebh", [128, 5, 512], BF16, "A")
    ebh_r = [P.res(f"ebh{i}") for i in range(5)]
    phase_res["A"].extend(ebh_r)
    eb = [T(f"eb{i}", [128, 5, 512], BF16, "A") for i in range(2)]
    b31 = None

    def mm(bank, out_ap, lhsT, rhs, start, stop, reads):
        P.op("pe", lambda e: e.matmul(out_ap, lhsT=lhsT, rhs=rhs, start=start, stop=stop),
             reads, [r_PS[bank]])

    ps_rot = [0]

    def next_main():
        b = ps_rot[0] % 3
        ps_rot[0] += 1
        return b

    rot = {}

    def nxt(lst, key):
        i = rot.get(key, 0)
        rot[key] = i + 1
        return lst[i % len(lst)]

    def wslot_load(srcs, reads):
        sl = wslots[wctr[0] % NSLOT]
        wctr[0] += 1
        for vf, src in srcs:
            P.dma("sp", vf(sl.t), src, reads=reads, writes=[sl.r])
        return sl

    def v3(t, a, b):
        return t[:, 0:a * b].rearrange("p (a b) -> p a b", b=b)

    def norm_squares(x, chunks):
        for c in chunks:
            s = nxt(sqb, "sqb")
            P.op("act", lambda e, s=s, c=c: e.activation(out=s.t[:], in_=x.t[:, c, :], func=AF.Square),
                 [x.r], [s.r])
            mm(3, PS[3][:], ones_bf.t[:], s.t[:], c == 0, c == 7, [ones_bf.r, s.r])

    def norm_finish(x, gain_t, gcol0):
        a = nxt(t1, "t1")
        rs = nxt(rstd, "rstd")
        P.op("act", lambda e: e.activation(out=a.t[:], in_=PS[3][:], func=AF.Sqrt, bias=epsc.t[:, 0:1], scale=1.0 / DM),
             [r_PS[3], epsc.r], [a.r])
        P.op("dve", lambda e: e.reciprocal(out=rs.t[:], in_=a.t[:]), [a.r], [rs.r])
        for c in range(8):
            P.op("dve", lambda e, c=c: e.scalar_tensor_tensor(out=hT.t[:, c, :], in0=x.t[:, c, :],
                                                              scalar=gain_t.t[:, gcol0 + c:gcol0 + c + 1], in1=rs.t[:],
                                                              op0=ALU.mult, op1=ALU.mult),
                 [x.r, gain_t.r, rs.r], [hT.rc[c]])

    def norm_fm(x, gain_t, gcol0):
        norm_squares(x, range(8))
        norm_finish(x, gain_t, gcol0)

    def headnorm(bank, npart, blk_t, scale, bias_col, gain_t, gcol, out_ap, out_res, extra_f32=None):
        s = nxt(sqb, "sqb")
        P.op("act", lambda e: e.activation(out=s.t[0:npart, :], in_=PS[bank][0:npart, :], func=AF.Square),
             [r_PS[bank]], [s.r])
        mm(4, PS[4][0:npart, :], blk_t.t[0:npart, 0:npart], s.t[0:npart, :], True, True, [blk_t.r, s.r])
        a = nxt(t1, "t1")
        rs = nxt(rstd, "rstd")
        P.op("act", lambda e: e.activation(out=a.t[0:npart, :], in_=PS[4][0:npart, :], func=AF.Sqrt,
                                           bias=epsc.t[0:npart, bias_col:bias_col + 1], scale=scale),
             [r_PS[4], epsc.r], [a.r])
        P.op("dve", lambda e: e.reciprocal(out=rs.t[0:npart, :], in_=a.t[0:npart, :]), [a.r], [rs.r])
        if extra_f32 is not None:
            P.op("dve", lambda e: e.scalar_tensor_tensor(out=extra_f32.t[0:npart, :], in0=PS[bank][0:npart, :],
                                                         scalar=gain_t.t[0:npart, gcol:gcol + 1], in1=rs.t[0:npart, :],
                                                         op0=ALU.mult, op1=ALU.mult),
                 [r_PS[bank], gain_t.r, rs.r], [extra_f32.r])
            if out_ap is not None:
                P.op("pool", lambda e: e.tensor_copy(out=out_ap, in_=extra_f32.t[0:npart, :]), [extra_f32.r], [out_res])
        else:
            P.op("dve", lambda e: e.scalar_tensor_tensor(out=out_ap, in0=PS[bank][0:npart, :],
                                                         scalar=gain_t.t[0:npart, gcol:gcol + 1], in1=rs.t[0:npart, :],
                                                         op0=ALU.mult, op1=ALU.mult),
                 [r_PS[bank], gain_t.r, rs.r], [out_res])

    def proj_fm(bank, M, lhs_fn, lhs_res, rhs_t=None, nk=8):
        rhs_t = rhs_t or hT
        for kc in range(nk):
            rr = rhs_t.rc[kc] if hasattr(rhs_t, "rc") else rhs_t.r
            mm(bank, PS[bank][0:M, :], lhs_fn(kc), rhs_t.t[:, kc, :], kc == 0, kc == nk - 1, [lhs_res, rr])

    def store_heads(dst, tile, h0, tt, res_list):
        for hh in range(2):
            P.dma("pool", dst[h0 + hh, 0:64, tt * 512:(tt + 1) * 512], tile.t[hh * 64:(hh + 1) * 64, :],
                  reads=[tile.r], writes=[res_list[h0 + hh][tt]])

    def inproj_even(l, tt):
        win = WIN[l]
        ct = colt[l]
        for grp, ci in (("qa", 0), ("ka", 1), ("qb", 4), ("kb", 5)):
            sl, w3 = win.load(ci)

            def fin(ch, b, grp=grp):
                o = nxt(obf, "obf")
                if grp == "qa":
                    headnorm(b, 128, blk64, 1.0, 1, ct, 16, o.t[:], o.r)
                    store_heads(QT, o, 2 * ch, tt, r_QT)
                elif grp == "ka":
                    headnorm(b, 128, blk64, 1.0 / 64, 0, ct, 17, o.t[:], o.r)
                    store_heads(KT, o, 2 * ch, tt, r_KT)
                elif grp == "qb":
                    P.op("act", lambda e, b=b, o=o: e.activation(out=o.t[:], in_=PS[b][:], func=AF.Copy, scale=0.125),
                         [r_PS[b]], [o.r])
                    store_heads(QT, o, 8 + 2 * ch, tt, r_QT)
                else:
                    P.op("act", lambda e, b=b, o=o: e.activation(out=o.t[:], in_=PS[b][:], func=AF.Copy),
                         [r_PS[b]], [o.r])
                    store_heads(KT, o, 8 + 2 * ch, tt, r_KT)

            pend = None
            for ch in range(4):
                b = next_main()
                proj_fm(b, 128, lambda kc, ch=ch: w3[:, kc, ch * 128:(ch + 1) * 128], sl.r)
                if pend is not None:
                    fin(*pend)
                pend = (ch, b)
            fin(*pend)
        for g, ci in ((0, 2), (1, 6)):
            sl, w3 = win.load(ci)
            for sub in range(4):
                b = next_main()
                for kc in range(8):
                    mm(b, PS[b][:], hT.t[:, kc, sub * 128:(sub + 1) * 128], w3[:, kc, :], kc == 0, kc == 7, [hT.rc[kc], sl.r])
                o = nxt(obf, "obf")
                if sub % 2 == 0:
                    P.op("act", lambda e, b=b, o=o: e.activation(out=o.t[:], in_=PS[b][:], func=AF.Copy), [r_PS[b]], [o.r])
                else:
                    P.op("dve", lambda e, b=b, o=o: e.tensor_copy(out=o.t[:], in_=PS[b][:]), [r_PS[b]], [o.r])
                P.dma("pool", VS[tt * 512 + sub * 128: tt * 512 + (sub + 1) * 128, g * 512:(g + 1) * 512], o.t[:],
                      reads=[o.r], writes=[r_VS[g][tt]])
        sl, w3 = win.load(3)
        for sub in range(4):
            for kc in range(8):
                mm(5, PS[5][:, sub * 8:(sub + 1) * 8], hT.t[:, kc, sub * 128:(sub + 1) * 128], w3[:, kc, :],
                   kc == 0, kc == 7, [hT.rc[kc], sl.r])
        for sub in range(4):
            P.op("dve", lambda e, sub=sub: e.tensor_tensor(out=nlf.t[:, sub, :], in0=PS[5][:, sub * 8:(sub + 1) * 8],
                                                           in1=ct.t[:, 24:32], op=ALU.add), [r_PS[5], ct.r], [nlf.r])
        P.op("act", lambda e: e.activation(out=nlf.t[:], in_=nlf.t[:], func=AF.Exp, scale=-1.0), [nlf.r], [nlf.r])
        P.op("act", lambda e: e.activation(out=nlf.t[:], in_=nlf.t[:], func=AF.Ln, bias=epsc.t[:, 3:4], scale=1.0),
             [nlf.r, epsc.r], [nlf.r])
        for sub in range(4):
            mm(6, PS[6][0:8, :], nlf.t[:, sub, :], cumc.t[:, sub, :], sub == 0, sub == 3, [nlf.r, cumc.r])
        if tt == 0:
            P.op("dve", lambda e: e.memset(carry.t[:], 0.0), [], [carry.r])
        P.op("dve", lambda e: e.tensor_scalar(out=cT.t[:], in0=PS[6][0:8, :], scalar1=carry.t[:, 0:1], scalar2=-1.0,
                                              op0=ALU.add, op1=ALU.mult), [r_PS[6], carry.r], [cT.r])
        P.op("dve", lambda e: e.tensor_scalar(out=carry.t[:], in0=cT.t[:, 511:512], scalar1=-1.0, scalar2=None,
                                              op0=ALU.mult), [cT.r], [carry.r])
        P.op("dve", lambda e: e.tensor_copy(out=csp[0].t[:], in_=cT.t[:]), [cT.r], [csp[0].r])
        P.op("dve", lambda e: e.tensor_tensor(out=crem[0].t[:], in0=cT.t[:], in1=csp[0].t[:], op=ALU.subtract),
             [cT.r, csp[0].r], [crem[0].r])
        P.op("dve", lambda e: e.tensor_copy(out=csp[1].t[:], in_=crem[0].t[:]), [crem[0].r], [csp[1].r])
        P.op("dve", lambda e: e.tensor_tensor(out=crem[1].t[:], in0=crem[0].t[:], in1=csp[1].t[:], op=ALU.subtract),
             [crem[0].r, csp[1].r], [crem[1].r])
        P.op("dve", lambda e: e.tensor_copy(out=csp[2].t[:], in_=crem[1].t[:]), [crem[1].r], [csp[2].r])
        for k3 in range(3):
            P.dma("pool", dap(QT, (64 + k3) * S + tt * 512, [[96 * S, 8], [1, 512]]), csp[k3].t[:],
                  reads=[csp[k3].r], writes=[r_QT[h][tt] for h in range(8)])
            P.dma("pool", dap(KT, (67 + k3) * S + tt * 512, [[96 * S, 8], [1, 512]]), csp[k3].t[:],
                  reads=[csp[k3].r], writes=[r_KT[h][tt] for h in range(8)])
        for k3 in range(3):
            P.dma("pool", dap(QT, (67 + k3) * S + tt * 512, [[96 * S, 8], [1, 512]]), negrow.t[:],
                  reads=[negrow.r], writes=[r_QT[h][tt] for h in range(8)])
            P.dma("pool", dap(KT, (64 + k3) * S + tt * 512, [[96 * S, 8], [1, 512]]), onesrow.t[:],
                  reads=[onesrow.r], writes=[r_KT[h][tt] for h in range(8)])

    def inproj_odd(l, tt):
        i = l // 2
        win = WIN[l]
        ct = colt[l]
        if tt == 0:
            P.dma("sp", wqb_sb.t[:], WQB[i], reads=r_W[f"wqb{i}"], writes=[wqb_sb.r])
            P.dma("sp", wkvb_sb.t[:], WKVB[i], reads=[r_W[f"wkvb{i}"]], writes=[wkvb_sb.r])
            P.op("dve", lambda e: e.memset(gsb.t[:], -1e30), [], [gsb.r])
            P.op("dve", lambda e: e.memset(selm1.t[:], 0.0), [], [selm1.r])
            P.op("dve", lambda e: e.memset(KM.t[:], 0.0), [], [KM.r])
        P.dma("pool", ropc.t[:], C["rope_c"][:, tt * 512:(tt + 1) * 512], writes=[ropc.r])
        P.dma("pool", rops.t[:], C["rope_s"][:, tt * 512:(tt + 1) * 512], writes=[rops.r])
        if "k" in parts:
            sl, w3 = win.load(1)

            def fink(ch, b):
                o = nxt(obf, "obf")
                headnorm(b, 128, blk64, 1.0 / 64, 0, ct, 17, o.t[:], o.r)
                store_heads(KT, o, 2 * ch, tt, r_KT)
                P.op("dve", lambda e, o=o, ch=ch: e.tensor_reduce(out=KM.t[:, ch, 2 * tt:2 * tt + 2],
                                                                  in_=o.t[:].rearrange("p (a b) -> p a b", b=256),
                                                                  axis=AX.X, op=ALU.add), [o.r], [KM.r])

            pend = None
            for ch in range(4):
                b = next_main()
                proj_fm(b, 128, lambda kc, ch=ch: w3[:, kc, ch * 128:(ch + 1) * 128], sl.r)
                if pend is not None:
                    fink(*pend)
                pend = (ch, b)
            fink(*pend)
        if "q" in parts:
            sl, w3 = win.load(0)

            def gates(ch, qq):
                for sub in range(4):
                    if (4 * tt + sub) // 2 <= 3:
                        continue
                    for hh in range(2):
                        h = 2 * ch + hh
                        mm(5, PS[5][:, sub * 128 + h * 16: sub * 128 + (h + 1) * 16],
                           qq.t[hh * 64:(hh + 1) * 64, sub * 128:(sub + 1) * 128], KM.t[hh * 64:(hh + 1) * 64, ch, :], True, True,
                           [qq.r, KM.r])

            def finq(ch, b):
                o = nxt(obf, "obf")
                qq = nxt(qf, "qf")
                headnorm(b, 128, blk64, 1.0, 1, ct, 16, o.t[:], o.r, extra_f32=qq)
                store_heads(QT, o, 2 * ch, tt, r_QT)
                return qq

            pend = None
            gpend = None
            for ch in range(4):
                b = next_main()
                proj_fm(b, 128, lambda kc, ch=ch: w3[:, kc, ch * 128:(ch + 1) * 128], sl.r)
                if gpend is not None:
                    gates(*gpend)
                    gpend = None
                if pend is not None:
                    gpend = (pend[0], finq(*pend))
                pend = (ch, b)
            if gpend is not None:
                gates(*gpend)
            gpend = (pend[0], finq(*pend))
            gates(*gpend)
        if "s" in parts:
            for sub in range(4):
                cur = (4 * tt + sub) // 2
                trs = slice(sub * 128, (sub + 1) * 128)
                if cur <= 3:
                    P.op("pe", lambda e, trs=trs: e.transpose(PS[6][:, trs], zero128.t[:], ident.t[:]),
                         [zero128.r, ident.r], [r_PS[6]])
                    continue
                g3 = PS[5][:, trs].rearrange("p (h n) -> p h n", n=16)
                P.op("dve", lambda e, g3=g3, cur=cur: e.tensor_copy(out=gsb.t[:, :, 0:cur], in_=g3[:, :, 0:cur]),
                     [r_PS[5]], [gsb.r])
                for h in range(8):
                    P.op("dve", lambda e, h=h: e.max(out=mx8.t[:, h, :], in_=gsb.t[:, h, :]), [gsb.r], [mx8.r])
                m0 = mx8.t[:, :, 2:3]
                thr = bass.AP(tensor=m0.tensor, offset=m0.offset, ap=[list(m0.ap[0]), list(m0.ap[1]), [0, 16]])
                P.op("dve", lambda e, thr=thr: e.tensor_tensor(out=selt.t[:], in0=gsb.t[:], in1=thr, op=ALU.is_ge),
                     [gsb.r, mx8.r], [selt.r])
                P.op("dve", lambda e, cur=cur: e.tensor_scalar(out=selm1.t[:, :, 0:cur], in0=selt.t[:, :, 0:cur],
                                                               scalar1=-1.0, scalar2=None, op0=ALU.add),
                     [selt.r], [selm1.r])
                P.op("pe", lambda e, trs=trs: e.transpose(PS[6][:, trs], selm1.t[:].rearrange("p h n -> p (h n)"), ident.t[:]),
                     [selm1.r, ident.r], [r_PS[6]])
            P.op("act", lambda e: e.activation(out=seltT.t[:], in_=PS[6][:], func=AF.Copy), [r_PS[6]], [seltT.r])
            P.dma("pool", dap(QT, 64 * S + tt * 512, [[96 * S, 8], [S, 16], [1, 512]]), seltT.t[:],
                  reads=[seltT.r], writes=[r_QT[h][tt] for h in range(8)])
        if "v" in parts:
            sl, w3 = win.load(2)
            for sub in range(4):
                b = next_main()
                for kc in range(8):
                    mm(b, PS[b][:], hT.t[:, kc, sub * 128:(sub + 1) * 128], w3[:, kc, :], kc == 0, kc == 7, [hT.rc[kc], sl.r])
                o = nxt(obf, "obf")
                P.op("act", lambda e, b=b, o=o: e.activation(out=o.t[:], in_=PS[b][:], func=AF.Copy), [r_PS[b]], [o.r])
                P.dma("pool", VS[tt * 512 + sub * 128: tt * 512 + (sub + 1) * 128, 0:512], o.t[:],
                      reads=[o.r], writes=[r_VS[0][tt]])
        if "l" in parts:
            sl, w3 = win.load(3)
            bq = [next_main(), next_main()]
            for c2 in range(2):
                proj_fm(bq[c2], 128, lambda kc, c2=c2: w3[:, kc, c2 * 128:(c2 + 1) * 128], sl.r)
                s = nxt(sqb, "sqb")
                P.op("act", lambda e, s=s, b=bq[c2]: e.activation(out=s.t[:], in_=PS[b][:], func=AF.Square), [r_PS[bq[c2]]], [s.r])
                mm(4, PS[4][:], ones_bf.t[:], s.t[:], c2 == 0, c2 == 1, [ones_bf.r, s.r])
            a = nxt(t1, "t1")
            rs = nxt(rstd, "rstd")
            P.op("act", lambda e: e.activation(out=a.t[:], in_=PS[4][:], func=AF.Sqrt, bias=epsc.t[:, 0:1], scale=1.0 / 256),
                 [r_PS[4], epsc.r], [a.r])
            P.op("dve", lambda e: e.reciprocal(out=rs.t[:], in_=a.t[:]), [a.r], [rs.r])
            for c2 in range(2):
                P.op("dve", lambda e, c2=c2, b=bq[c2]: e.scalar_tensor_tensor(out=qln.t[:, c2, :], in0=PS[b][:],
                                                                              scalar=ct.t[:, 18 + c2:19 + c2], in1=rs.t[:],
                                                                              op0=ALU.mult, op1=ALU.mult),
                     [r_PS[bq[c2]], ct.r, rs.r], [qln.r])
            b = next_main()
            proj_fm(b, 128, lambda kc: w3[:, kc, 256:384], sl.r)
            headnorm(b, 128, ones_bf, 1.0 / 128, 0, ct, 20, kvn.t[:], kvn.r)
            proj_fm(7, 96, lambda kc: w3[:, kc, 320:416], sl.r)
            for kd in kdr:
                P.op("act", lambda e, kd=kd: e.activation(out=kd.t[64:96, :], in_=PS[7][64:96, :], func=AF.Copy), [r_PS[7]], [kd.r])
            for sq in sq96s:
                P.op("act", lambda e, sq=sq: e.activation(out=sq.t[64:96, :], in_=PS[7][64:96, :], func=AF.Square), [r_PS[7]], [sq.r])

        def rope_store(y, dst, hidx, rl):
            mm(7, PS[7][0:96, :], perm.t[:], y.t[:], True, True, [perm.r, y.r])
            P.op("pool", lambda e: e.tensor_tensor(out=rt1.t[:], in0=y.t[:], in1=ropc.t[:], op=ALU.mult),
                 [y.r, ropc.r], [rt1.r])
            P.op("dve", lambda e: e.tensor_tensor(out=rt2.t[:], in0=PS[7][0:96, :], in1=rops.t[:], op=ALU.mult),
                 [r_PS[7], rops.r], [rt2.r])
            o = nxt(obf, "obf")
            P.op("dve", lambda e: e.tensor_tensor(out=o.t[0:96, :], in0=rt1.t[:], in1=rt2.t[:], op=ALU.add),
                 [rt1.r, rt2.r], [o.r])
            P.dma("pool", dst[hidx, 0:96, tt * 512:(tt + 1) * 512], o.t[0:96, :], reads=[o.r], writes=[rl[hidx][tt]])

        if "Q" in parts:
            pend = None
            for h in range(8):
                b = next_main()
                for kc in range(2):
                    mm(b, PS[b][0:96, :], wqb_sb.t[:, kc, h * 96:(h + 1) * 96], qln.t[:, kc, :], kc == 0, kc == 1,
                       [wqb_sb.r, qln.r])
                y = nxt(yf, "yf")
                headnorm(b, 96, ones_bf, 1.0, 2, ct, 21, None, None, extra_f32=y)
                if pend is not None:
                    rope_store(pend[0], QT, 8 + pend[1], r_QT)
                pend = (y, h)
            rope_store(pend[0], QT, 8 + pend[1], r_QT)
        if "K" in parts:
            pend = None
            for h in range(8):
                b = next_main()
                mm(b, PS[b][0:64, :], wkvb_sb.t[:, h * 128:h * 128 + 64], kvn.t[:], True, True, [wkvb_sb.r, kvn.r])
                kd = nxt(kdr, "kdr")
                sq = nxt(sq96s, "sq96s")
                P.op("act", lambda e, b=b, kd=kd: e.activation(out=kd.t[0:64, :], in_=PS[b][0:64, :], func=AF.Copy),
                     [r_PS[b]], [kd.r])
                P.op("act", lambda e, b=b, sq=sq: e.activation(out=sq.t[0:64, :], in_=PS[b][0:64, :], func=AF.Square),
                     [r_PS[b]], [sq.r])
                mm(4, PS[4][0:96, :], ones_bf.t[0:96, 0:96], sq.t[:], True, True, [ones_bf.r, sq.r])
                a = nxt(t1, "t1")
                rs = nxt(rstd, "rstd")
                P.op("act", lambda e, a=a: e.activation(out=a.t[0:96, :], in_=PS[4][0:96, :], func=AF.Sqrt,
                                                        bias=epsc.t[0:96, 0:1], scale=1.0 / 96), [r_PS[4], epsc.r], [a.r])
                P.op("dve", lambda e, a=a, rs=rs: e.reciprocal(out=rs.t[0:96, :], in_=a.t[0:96, :]), [a.r], [rs.r])
                y = nxt(yf, "yf")
                P.op("dve", lambda e, y=y, rs=rs, kd=kd: e.scalar_tensor_tensor(out=y.t[:], in0=kd.t[:], scalar=ct.t[0:96, 22:23],
                                                                                in1=rs.t[0:96, :], op0=ALU.mult, op1=ALU.mult),
                     [kd.r, ct.r, rs.r], [y.r])
                if pend is not None:
                    rope_store(pend[0], KT, 8 + pend[1], r_KT)
                pend = (y, h)
            rope_store(pend[0], KT, 8 + pend[1], r_KT)
        if "V" in parts:
            wv = wkvb_sb.t[:].rearrange("p (h c) -> p h c", c=128)[:, :, 64:128]
            for sub in range(4):
                b = next_main()
                mm(b, PS[b][:].rearrange("p (h c) -> p h c", c=64), kvn.t[:, sub * 128:(sub + 1) * 128], wv, True, True,
                   [kvn.r, wkvb_sb.r])
                o = nxt(obf, "obf")
                P.op("act", lambda e, b=b, o=o: e.activation(out=o.t[:], in_=PS[b][:], func=AF.Copy), [r_PS[b]], [o.r])
                P.dma("pool", VS[tt * 512 + sub * 128: tt * 512 + (sub + 1) * 128, 512:1024], o.t[:],
                      reads=[o.r], writes=[r_VS[1][tt]])

    def outproj_ffn(l, tt, x, mixT, ln):
        for half in range(2):
            sl, w3 = WOUT[l].load(half)
            for ch in range(4):
                dc = half * 4 + ch
                b = next_main()
                proj_fm(b, 128, lambda kc, ch=ch: w3[:, kc, ch * 128:(ch + 1) * 128], sl.r, rhs_t=mixT)
                P.op("dve", lambda e, b=b, dc=dc: e.tensor_tensor(out=x.t[:, dc, :], in0=PS[b][:], in1=x.t[:, dc, :],
                                                                  op=ALU.add), [r_PS[b], x.r], [x.r])
            norm_squares(x, range(half * 4, half * 4 + 4))
        norm_finish(x, colt[l], 8)
        for gi, (f0, f1) in enumerate(FGRP):
            slg, wg3 = WGU[l].load(gi)
            slu, wu3 = WGU[l].load(len(FGRP) + gi)
            for fl in range(f1 - f0):
                fc = f0 + fl
                bg = 5 + (fc % 2)
                bu = next_main()
                for kc in range(8):
                    mm(bg, PS[bg][:], wg3[:, kc, fl * 128:(fl + 1) * 128], hT.t[:, kc, :], kc == 0, kc == 7, [slg.r, hT.rc[kc]])
                for kc in range(8):
                    mm(bu, PS[bu][:], wu3[:, kc, fl * 128:(fl + 1) * 128], hT.t[:, kc, :], kc == 0, kc == 7, [slu.r, hT.rc[kc]])
                s = nxt(sg, "sg")
                P.op("act", lambda e, s=s, bg=bg: e.activation(out=s.t[:], in_=PS[bg][:], func=AF.Silu), [r_PS[bg]], [s.r])
                P.op("dve", lambda e, s=s, bu=bu, fc=fc: e.tensor_tensor(out=actT.t[:, fc, :], in0=PS[bu][:], in1=s.t[:],
                                                                         op=ALU.mult), [r_PS[bu], s.r], [actT.r])
        dbanks = [0, 1, 2, 7]
        for dh in range(2):
            for g, (f0, f1) in enumerate(DGRP):
                sl, w3 = WD[l][g].load(dh)
                for fl in range(f1 - f0):
                    fc = f0 + fl
                    for d4 in range(4):
                        b = dbanks[d4]
                        mm(b, PS[b][:], w3[:, fl, d4 * 128:(d4 + 1) * 128], actT.t[:, fc, :], fc == 0, fc == NF - 1,
                           [sl.r, actT.r])
            for d4 in range(4):
                b = dbanks[d4]
                dc = dh * 4 + d4
                P.op("dve", lambda e, b=b, dc=dc: e.tensor_tensor(out=x.t[:, dc, :], in0=PS[b][:], in1=x.t[:, dc, :], op=ALU.add),
                     [r_PS[b], x.r], [x.r])
            if ln is not None:
                norm_squares(x, range(dh * 4, dh * 4 + 4))
        if ln is not None:
            norm_finish(x, colt[ln], 0)

    out_ops = []

    def r_phase(lp, ln):
        for tt in range(NT):
            x = xt[tt % 2]
            if lp is None:
                for sub in range(4):
                    tk = nxt(tok, "tok")
                    r0 = tt * 512 + sub * 128
                    P.dma("pool", tk.t[:], x_in[r0:r0 + 128, :], writes=[tk.r])
                    for half in range(2):
                        bb = 5 + half
                        for c4 in range(4):
                            c = half * 4 + c4
                            P.op("pe", lambda e, bb=bb, c4=c4, c=c, tk=tk: e.transpose(PS[bb][:, c4 * 128:(c4 + 1) * 128],
                                                                                       tk.t[:, c * 128:(c + 1) * 128], ident.t[:]),
                                 [tk.r, ident.r], [r_PS[bb]])
                        P.op("dve" if half == 0 else "act",
                             (lambda e, bb=bb, half=half, sub=sub: e.tensor_copy(
                                 out=x.t[:, half * 4:(half + 1) * 4, sub * 128:(sub + 1) * 128],
                                 in_=PS[bb][:].rearrange("p (c t) -> p c t", t=128))) if half == 0 else
                             (lambda e, bb=bb, half=half, sub=sub: e.activation(
                                 out=x.t[:, half * 4:(half + 1) * 4, sub * 128:(sub + 1) * 128],
                                 in_=PS[bb][:].rearrange("p (c t) -> p c t", t=128), func=AF.Copy)),
                             [r_PS[bb]], [x.r])
            else:
                def _loads(t2):
                    P.dma("pool", xt[t2 % 2].t[:], XT[:, :, t2 * 512:(t2 + 1) * 512], reads=[r_XT[t2]], writes=[xt[t2 % 2].r])
                    P.dma("pool", mixTs[t2 % 2].t[:], MIX[:, :, t2 * 512:(t2 + 1) * 512], reads=[r_MIX[t2]],
                          writes=[mixTs[t2 % 2].r])
                if tt == 0:
                    _loads(0)
                if tt + 1 < NT:
                    _loads(tt + 1)
                outproj_ffn(lp, tt, x, mixTs[tt % 2], ln)
            if ln is not None:
                if lp is None:
                    norm_fm(x, colt[ln], 0)
                if ln % 2 == 0:
                    inproj_even(ln, tt)
                else:
                    inproj_odd(ln, tt)
                P.dma("pool", XT[:, :, tt * 512:(tt + 1) * 512], x.t[:], reads=[x.r], writes=[r_XT[tt]])
            else:
                for sub in range(4):
                    tk = nxt(tok, "tok")
                    for half in range(2):
                        bb = 5 + half
                        for c4 in range(4):
                            c = half * 4 + c4
                            P.op("pe", lambda e, bb=bb, c4=c4, c=c, sub=sub: e.transpose(
                                PS[bb][:, c4 * 128:(c4 + 1) * 128], x.t[:, c, sub * 128:(sub + 1) * 128], ident.t[:]),
                                 [x.r, ident.r], [r_PS[bb]])
                        if half == 0:
                            P.op("dve", lambda e, bb=bb, tk=tk: e.tensor_copy(out=tk.t[:, 0:512], in_=PS[bb][:]),
                                 [r_PS[bb]], [tk.r])
                        else:
                            P.op("act", lambda e, bb=bb, tk=tk: e.activation(out=tk.t[:, 512:1024], in_=PS[bb][:], func=AF.Copy),
                                 [r_PS[bb]], [tk.r])
                    r0 = tt * 512 + sub * 128
                    out_ops.append(P.dma("pool", out_d[r0:r0 + 128, :], tk.t[:], reads=[tk.r]))

    def load_head(hg, K, buf, moba_h=None):
        q, k, v = qsb[buf], ksb[buf], vaug[buf]
        P.dma("pool", q.t[0:K, :], QT[hg, 0:K, :], reads=r_QT[hg], writes=[q.r])
        P.dma("pool", k.t[0:K, :], KT[hg, 0:K, :], reads=r_KT[hg] + [r_KTaug[hg]], writes=[k.r])
        g = hg // 8
        P.dma("pool", v.t[:, :, 0:64], dap(VS, hg * 64, [[DM, 128], [128 * DM, 32], [1, 64]]),
              reads=r_VS[g], writes=[v.r])
        if moba_h is not None:
            for d5 in range(5):
                dd = 128 - 128 * d5
                P.dma("pool", ebh.t[:, d5, :], dap(GSR, moba_h * GLEN + (128 - dd), [[1, 128], [1, 512]]),
                      reads=[r_GSR], writes=[ebh_r[d5]])
            e0 = ebh.t[:, :, :]
            rev = bass.AP(tensor=e0.tensor, offset=e0.offset + 511, ap=[list(e0.ap[0]), list(e0.ap[1]), [-1, 512]])
            P.op("pool", lambda e: e.tensor_copy(out=eb[buf].t[:], in_=rev), ebh_r, [eb[buf].r])

    def att_softmax(hg, K, buf, kind, bias_t=None, moba_h=None):
        q, k, v = qsb[buf], ksb[buf], vaug[buf]
        SBK = [(0, 1), (2, 3)]
        OBK = [4, 5]

        def s_mm(qt, kb, b):
            diag = kind != "moba" and kb >= 4 * qt
            mm(b, PS[b][:], k.t[0:K, kb * 128:(kb + 1) * 128], q.t[0:K, qt * 512:(qt + 1) * 512], True, not diag, [k.r, q.r])
            if diag:
                mm(b, PS[b][:], ident_bf.t[:], mask_le.t[:, kb - 4 * qt, :], False, True, [ident_bf.r, mask_le.r])

        for qt in range(NT):
            nkb = 4 * qt + 4
            npair = nkb // 2
            bo = OBK[qt % 2]

            def s_pair(pi):
                for j in range(2):
                    s_mm(qt, 2 * pi + j, SBK[pi % 2][j])

            s_pair(0)
            for pi in range(npair):
                if pi + 1 < npair:
                    s_pair(pi + 1)
                b0, b1 = SBK[pi % 2]
                kbp = (2 * pi, 2 * pi + 1)
                near = [kind == "moba" and kb >= 4 * qt - 1 for kb in kbp]
                if not any(near):
                    p = nxt(pt2, "pt2")
                    src = PSBIG[:, b0 * 512:(b0 + 2) * 512]
                    if kind == "moba":
                        P.op("act", lambda e, p=p, src=src: e.activation(out=p.t[:], in_=src, func=AF.Exp,
                                                                         bias=bias_t.t[:, 24 + moba_h:25 + moba_h], scale=1.0),
                             [r_PS[b0], r_PS[b1], bias_t.r], [p.r])
                    else:
                        P.op("act", lambda e, p=p, src=src: e.activation(out=p.t[:], in_=src, func=AF.Exp),
                             [r_PS[b0], r_PS[b1]], [p.r])
                    for j in range(2):
                        kb = kbp[j]
                        mm(bo, PS[bo][:], v.t[:, kb, :], p.t[:, j * 512:(j + 1) * 512], kb == 0, kb == nkb - 1, [v.r, p.r])
                else:
                    for j in range(2):
                        kb = kbp[j]
                        b = (b0, b1)[j]
                        p = nxt(pt, "pt")
                        if near[j]:
                            f = nxt(ef, "ef")
                            d5 = kb - (4 * qt - 1)
                            P.op("act", lambda e, b=b, f=f: e.activation(out=f.t[:], in_=PS[b][:], func=AF.Exp), [r_PS[b]], [f.r])
                            P.op("dve", lambda e, f=f, p=p, d5=d5: e.tensor_tensor(out=p.t[:], in0=f.t[:], in1=eb[buf].t[:, d5, :],
                                                                                   op=ALU.mult), [f.r, eb[buf].r], [p.r])
                        else:
                            P.op("act", lambda e, b=b, p=p: e.activation(out=p.t[:], in_=PS[b][:], func=AF.Exp,
                                                                         bias=bias_t.t[:, 24 + moba_h:25 + moba_h], scale=1.0),
                                 [r_PS[b], bias_t.r], [p.r])
                        mm(bo, PS[bo][:], v.t[:, kb, :], p.t[:], kb == 0, kb == nkb - 1, [v.r, p.r])
            rc = rec[qt % 2]
            o = ot[qt % 2]
            P.op("dve", lambda e, rc=rc, bo=bo: e.reciprocal(out=rc.t[:], in_=PS[bo][64:128, :]), [r_PS[bo]], [rc.r])
            P.op("dve", lambda e, rc=rc, bo=bo, o=o: e.tensor_tensor(out=o.t[:], in0=PS[bo][0:64, :], in1=rc.t[:], op=ALU.mult),
                 [r_PS[bo], rc.r], [o.r])
            c, p0 = divmod(hg * 64, 128)
            P.dma("pool", MIX[p0:p0 + 64, c, qt * 512:(qt + 1) * 512], o.t[:], reads=[o.r], writes=[r_MIX[qt]])

    def att_sb(hg, buf):
        q, k, v = qsb[buf], ksb[buf], vaug[buf]
        ZB = [0, 1]
        TB = [4, 5]
        OB = [6, 7]
        for pair in range(NT // 2):
            qts = [2 * pair, 2 * pair + 1]
            nk = [4 * qt + 4 for qt in qts]
            kbs = [list(range(n - 1, -1, -1)) for n in nk]
            nmax = nk[1]

            def active(n):
                return [s_ for s_ in range(2) if n < nk[s_]]

            def rng(n):
                a_ = active(n)
                return (a_[0] * 512, (a_[-1] + 1) * 512)

            def emit_z(n):
                for s_ in active(n):
                    kb = kbs[s_][n]
                    mm(ZB[s_], PS[ZB[s_]][:], k.t[0:64, kb * 128:(kb + 1) * 128],
                       q.t[0:64, qts[s_] * 512:(qts[s_] + 1) * 512], True, True, [k.r, q.r])

            def emit_e(n):
                f = ef2[n % 2]
                lo, hi = rng(n)
                P.op("act", lambda e, f=f, lo=lo, hi=hi: e.activation(out=f.t[:, lo:hi], in_=PSBIG[:, lo:hi], func=AF.Exp),
                     [r_PS[ZB[s_]] for s_ in active(n)], [f.r])
                for s_ in active(n):
                    kb = kbs[s_][n]
                    if kb >= 4 * qts[s_]:
                        r = kb - 4 * qts[s_]
                        P.op("pool", lambda e, f=f, r=r, s_=s_: e.tensor_tensor(out=f.t[:, s_ * 512:(s_ + 1) * 512],
                                                                               in0=f.t[:, s_ * 512:(s_ + 1) * 512],
                                                                               in1=mask_lt.t[:, r, :], op=ALU.mult),
                             [f.r, mask_lt.r], [f.r])
                return f

            def emit_sp(n, f):
                sp_ = spb2[n % 2]
                lo, hi = rng(n)
                P.op("act", lambda e, f=f, sp_=sp_, lo=lo, hi=hi: e.activation(out=sp_.t[:, lo:hi], in_=f.t[:, lo:hi], func=AF.Ln,
                                                                               bias=epsc.t[:, 3:4], scale=1.0),
                     [f.r, epsc.r], [sp_.r])
                return sp_

            def emit_t1(n, sp_):
                for s_ in active(n):
                    mm(TB[s_], PS[TB[s_]][:], triinc.t[:], sp_.t[:, s_ * 512:(s_ + 1) * 512], n == 0, False, [triinc.r, sp_.r])

            def emit_g(n):
                g = gf2[n % 2]
                lo, hi = rng(n)
                P.op("act", lambda e, g=g, lo=lo, hi=hi: e.activation(out=g.t[:, lo:hi], in_=PSBIG[:, 4 * 512 + lo:4 * 512 + hi],
                                                                      func=AF.Exp, scale=-1.0),
                     [r_PS[TB[s_]] for s_ in active(n)], [g.r])
                return g

            def emit_t2(n, sp_):
                for s_ in active(n):
                    mm(TB[s_], PS[TB[s_]][:], tristr.t[:], sp_.t[:, s_ * 512:(s_ + 1) * 512], False, n == nk[s_] - 1,
                       [tristr.r, sp_.r])

            def emit_p(n, f, g):
                p = nxt(pt2, "pt2")
                lo, hi = rng(n)
                P.op("dve", lambda e, p=p, f=f, g=g, lo=lo, hi=hi: e.tensor_tensor(out=p.t[:, lo:hi], in0=f.t[:, lo:hi],
                                                                                  in1=g.t[:, lo:hi], op=ALU.mult),
                     [f.r, g.r], [p.r])
                for s_ in active(n):
                    kb = kbs[s_][n]
                    mm(OB[s_], PS[OB[s_]][0:64, :], v.t[:, kb, 0:64], p.t[:, s_ * 512:(s_ + 1) * 512], n == 0, n == nk[s_] - 1,
                       [v.r, p.r])

            emit_z(0)
            f = emit_e(0)
            sp_ = emit_sp(0, f)
            if 1 < nmax:
                emit_z(1)
            emit_t1(0, sp_)
            for n in range(nmax):
                f_next = sp_next = None
                if n + 1 < nmax:
                    f_next = emit_e(n + 1)
                if n + 2 < nmax:
                    emit_z(n + 2)
                g = emit_g(n)
                emit_t2(n, sp_)
                if n + 1 < nmax:
                    sp_next = emit_sp(n + 1, f_next)
                    emit_t1(n + 1, sp_next)
                emit_p(n, f, g)
                f, sp_ = f_next, sp_next
            for s_ in range(2):
                o = ot[s_]
                P.op("dve", lambda e, o=o, s_=s_: e.tensor_copy(out=o.t[:], in_=PS[OB[s_]][0:64, :]), [r_PS[OB[s_]]], [o.r])
                c, p0 = divmod(hg * 64, 128)
                P.dma("pool", MIX[p0:p0 + 64, c, qts[s_] * 512:(qts[s_] + 1) * 512], o.t[:], reads=[o.r],
                      writes=[r_MIX[qts[s_]]])

    def att_phase(l, after_first_loads=None):
        even = l % 2 == 0
        phase_barrier()
        for i in range(2):
            P.op("pool", lambda e, i=i: e.memset(vaug[i].t[:, :, 64:128], 1.0), [], [vaug[i].r])
        if even:
            plan = [(h, 70, "fox") for h in range(8)] + [(8 + h, 64, "sb") for h in range(8)]
        else:
            plan = [(h, 80, "moba") for h in range(8)] + [(8 + h, 96, "mla") for h in range(8)]
        plan = [p_ for p_ in plan if p_[2] not in skip_kinds]
        if l in skip_att:
            plan = []
        for idx, (hg, K, kind) in enumerate(plan):
            buf = idx % 2
            if idx == 0:
                load_head(hg, K, buf, moba_h=hg if kind == "moba" else None)
            if idx + 1 < len(plan):
                nh, nK, nkind = plan[idx + 1]
                load_head(nh, nK, 1 - buf, moba_h=nh if nkind == "moba" else None)
            if idx == 0 and after_first_loads is not None:
                after_first_loads()
            if kind == "sb":
                att_sb(hg, buf)
            else:
                att_softmax(hg, K, buf, kind, bias_t=colt[l], moba_h=hg if kind == "moba" else None)

    def build_gsr():
        rbT = T("rbT", [32, 8], F32)
        P.dma("pool", rbT.t[:], rbT_d, writes=[rbT.r])
        for j, (a, b_) in enumerate(((0, 512), (512, 1024), (1024, GLEN))):
            n = b_ - a
            oh, sm, gr = sg[0], sg[1], obf[0]
            P.dma("pool", oh.t[0:32, 0:n], C["ohr"][:, a:b_], writes=[oh.r])
            P.dma("pool", sm.t[0:8, 0:n], C["stepmask"][:, a:b_], writes=[sm.r])
            mm(5, PS[5][0:8, 0:n], rbT.t[:], oh.t[0:32, 0:n], True, True, [rbT.r, oh.r])
            P.op("act", lambda e, n=n: e.activation(out=t1[0].t[0:8, 0:n], in_=PS[5][0:8, 0:n], func=AF.Exp),
                 [r_PS[5]], [t1[0].r])
            P.op("dve", lambda e, n=n, sm=sm, gr=gr: e.tensor_tensor(out=gr.t[0:8, 0:n], in0=t1[0].t[0:8, 0:n],
                                                                     in1=sm.t[0:8, 0:n], op=ALU.mult),
                 [t1[0].r, sm.r], [gr.r])
            P.dma("pool", GSR[:, a:b_], gr.t[0:8, 0:n], reads=[gr.r], writes=[r_GSR])

    def moba_kaug():
        kh = ksb[0]
        P.dma("pool", kh.t[0:16, :], C["kblockhot"], writes=[kh.r])
        for h in range(8):
            P.dma("pool", KT[h, 64:80, :], kh.t[0:16, :], reads=[kh.r], writes=[r_KTaug[h]])

    conv_layer(0)
    r_phase(None, 0)
    for l in range(n_layers):
        def _conv(l=l):
            if l == 0:
                conv_layer_post(0)
            if l + 1 < n_layers:
                conv_layer(l + 1)
                conv_layer_post(l + 1)
        att_phase(l, _conv)
        if l + 1 < n_layers and (l + 1) % 2 == 1:
            if "a" in parts:
                moba_kaug()
        phase_barrier()
        if l == 0 and n_layers > 1 and "g" in parts:
            build_gsr()
        r_phase(l, l + 1 if l + 1 < n_layers else None)
    cnt = P.finalize_and_emit(out_ops)
    st.close()
    return nc, cnt


_CACHE = {}


def make_in_maps(inputs, n_layers=4):
    consts = host_consts()
    shared = {}
    shared["ffn_w_gate_up"] = np.ascontiguousarray(inputs["ffn_w_gate_up"], np.float32)
    shared["ffn_w_down"] = np.ascontiguousarray(inputs["ffn_w_down"], np.float32)
    shared["ev_w_in"] = np.ascontiguousarray(inputs["ev_w_in"], np.float32)
    shared["ev_w_out"] = np.ascontiguousarray(inputs["ev_w_out"], np.float32)
    shared["od_w_in"] = np.ascontiguousarray(inputs["od_w_in"], np.float32)
    shared["od_w_out"] = np.ascontiguousarray(inputs["od_w_out"], np.float32)
    shared["od_mla_w_q_b"] = np.ascontiguousarray(inputs["od_mla_w_q_b"], np.float32)
    shared["od_mla_w_kv_b"] = np.ascontiguousarray(inputs["od_mla_w_kv_b"], np.float32)
    shared["cols"] = np.stack([host_cols(inputs, l) for l in range(4)]).astype(np.float32)
    shared["rbT"] = np.ascontiguousarray(np.asarray(inputs["rel_bias"], np.float32).T)
    for k, v in consts.items():
        shared["c_" + k] = np.ascontiguousarray(v, np.float32)
    x = np.asarray(inputs["x"], np.float32)
    maps = []
    for b in range(8):
        m = dict(shared)
        m["x"] = np.ascontiguousarray(x[b])
        maps.append(m)
    return maps


def kernel(**inputs):
    inputs = {k: np.asarray(v) for k, v in inputs.items()}
    if "nc" not in _CACHE:
        _CACHE["nc"] = build(4)[0]
    nc = _CACHE["nc"]
    maps = make_in_maps(inputs)
    res = run_bass_kernel_spmd(nc, maps, core_ids=list(range(8)))
    out = np.stack([np.asarray(r["out"], np.float32) for r in res.results], axis=0)
    return out
```
